# Optimizing a Trainium2 kernel written in Bass

```python
import jax, jax.numpy as jnp
from jax import lax
import numpy as np

D_MODEL = 1024
BATCH = 8
SEQ = 2048
DEPTH = 1
DEC_BATCH = 16
DEC_SEQ = 2048
PAST_LEN = 128

GRID_W = 64
EPS = 1e-6
ATT_HEADS = 8
ATT_KV_HEADS = 2
ATT_GROUP = ATT_HEADS // ATT_KV_HEADS
ATT_HEAD_DIM = 64
ROPE_AXIS_DIM = ATT_HEAD_DIM // 2
ROPE_THETA = 10000.0
Q_BLOCK = 128
M_HEADS = 4
M_HEAD_DIM = 128
M_CHUNK = 64
FORGET_BIAS_LO = 3.0
FORGET_BIAS_HI = 6.0
D_FF = 2816
ATT_WIDTH = ATT_HEADS * ATT_HEAD_DIM
KV_WIDTH = ATT_KV_HEADS * ATT_HEAD_DIM
M_WIDTH = M_HEADS * M_HEAD_DIM
N_GATE_COLS = 4 * M_HEADS
IN_SPLITS = (ATT_WIDTH, KV_WIDTH, KV_WIDTH, M_WIDTH, M_WIDTH, M_WIDTH, N_GATE_COLS, M_WIDTH, D_MODEL, D_MODEL)
D_IN = sum(IN_SPLITS)

kernel_name = 'hybrid_gqa_mlstm_macaron_encoder'


def rms_norm(x, g):
    xf = x.astype(jnp.float32)
    y = xf * lax.rsqrt(jnp.mean(xf * xf, axis=-1, keepdims=True) + EPS)
    return (y * g.astype(jnp.float32)).astype(x.dtype)


def swiglu(x, w_gate, w_up, w_down):
    return (jax.nn.silu(x @ w_gate) * (x @ w_up)) @ w_down


def axial_rope_angles(seq_len):
    rows = seq_len // GRID_W
    row = jnp.repeat(jnp.arange(rows, dtype=jnp.float32), GRID_W)
    col = jnp.tile(jnp.arange(GRID_W, dtype=jnp.float32), rows)
    inv_freq = ROPE_THETA ** (-jnp.arange(0, ROPE_AXIS_DIM, 2, dtype=jnp.float32) / ROPE_AXIS_DIM)
    ang = jnp.concatenate([row[:, None] * inv_freq, col[:, None] * inv_freq], axis=-1)
    return jnp.cos(ang), jnp.sin(ang)


def apply_rope(x, cos, sin):
    xf = x.astype(jnp.float32).reshape(x.shape[:-1] + (ATT_HEAD_DIM // 2, 2))
    x0, x1 = xf[..., 0], xf[..., 1]
    c, s = cos[None, :, None, :], sin[None, :, None, :]
    out = jnp.stack([x0 * c - x1 * s, x0 * s + x1 * c], axis=-1)
    return out.reshape(x.shape).astype(x.dtype)


def gqa_block_attention(q, k, v):
    B, S = q.shape[:2]
    nb = S // Q_BLOCK
    qb = q.reshape(B, nb, Q_BLOCK, ATT_KV_HEADS, ATT_GROUP, ATT_HEAD_DIM).transpose(1, 0, 2, 3, 4, 5)
    scale = ATT_HEAD_DIM ** -0.5

    def one_block(qi):
        s = jnp.einsum('bqkgd,bskd->bkgqs', qi, k, preferred_element_type=jnp.float32) * scale
        p = jax.nn.softmax(s, axis=-1).astype(v.dtype)
        return jnp.einsum('bkgqs,bskd->bqkgd', p, v)

    o = lax.map(one_block, qb)
    return o.transpose(1, 0, 2, 3, 4, 5).reshape(B, S, ATT_WIDTH)


def mlstm_direction(q, k, v, logi, logf):
    B, H, S, d = q.shape
    nc = S // M_CHUNK

    def chunks(a):
        a = a.reshape((B, H, nc, M_CHUNK) + a.shape[3:])
        return jnp.moveaxis(a, 2, 0)

    tril = jnp.tril(jnp.ones((M_CHUNK, M_CHUNK), dtype=bool))

    def step(carry, inp):
        C, n, m = carry
        qc, kc, vc, li, lf = inp
        b = jnp.cumsum(lf, axis=-1)
        D = b[..., :, None] - b[..., None, :] + li[..., None, :]
        D = jnp.where(tril, D, -jnp.inf)
        inter = b + m[..., None]
        m_t = jnp.maximum(inter, jnp.max(D, axis=-1))
        w_inter = jnp.exp(inter - m_t)
        S_qk = jnp.einsum('bhtd,bhsd->bhts', qc, kc) * jnp.exp(D - m_t[..., None])
        num = jnp.einsum('bhts,bhse->bhte', S_qk, vc) + w_inter[..., None] * jnp.einsum('bhtd,bhde->bhte', qc, C)
        den = jnp.sum(S_qk, axis=-1) + w_inter * jnp.einsum('bhtd,bhd->bht', qc, n)
        h = num / jnp.maximum(jnp.abs(den), jnp.exp(-m_t))[..., None]
        bL = b[..., -1]
        g = bL[..., None] - b + li
        m_new = jnp.maximum(bL + m, jnp.max(g, axis=-1))
        w_old = jnp.exp(bL + m - m_new)
        w_s = jnp.exp(g - m_new[..., None])
        C_new = w_old[..., None, None] * C + jnp.einsum('bhs,bhsd,bhse->bhde', w_s, kc, vc)
        n_new = w_old[..., None] * n + jnp.einsum('bhs,bhsd->bhd', w_s, kc)
        return (C_new, n_new, m_new), h

    init = (jnp.zeros((B, H, d, d), jnp.float32), jnp.zeros((B, H, d), jnp.float32), jnp.zeros((B, H), jnp.float32))
    _, h = lax.scan(step, init, (chunks(q), chunks(k), chunks(v), chunks(logi), chunks(logf)))
    return jnp.moveaxis(h, 0, 2).reshape(B, H, S, d)


def mlstm_bidirectional(q, k, v, gate_pre, gate_bias, o_pre, head_norm):
    B, S, _ = q.shape

    def to_heads(a):
        return a.astype(jnp.float32).reshape(B, S, M_HEADS, M_HEAD_DIM).transpose(0, 2, 1, 3)

    qh, kh, vh = to_heads(q), to_heads(k) * (M_HEAD_DIM ** -0.5), to_heads(v)
    gp = (gate_pre.astype(jnp.float32) + gate_bias.astype(jnp.float32)).reshape(B, S, 4, M_HEADS).transpose(2, 0, 3, 1)
    i_fwd, f_fwd = gp[0], jax.nn.log_sigmoid(gp[1])
    i_bwd, f_bwd = gp[2], jax.nn.log_sigmoid(gp[3])
    h_fwd = mlstm_direction(qh, kh, vh, i_fwd, f_fwd)

    def flip(a):
        return jnp.flip(a, axis=2)

    h_bwd = flip(mlstm_direction(flip(qh), flip(kh), flip(vh), flip(i_bwd), flip(f_bwd)))
    h = (h_fwd + h_bwd).transpose(0, 2, 1, 3)
    h = h * lax.rsqrt(jnp.mean(h * h, axis=-1, keepdims=True) + EPS)
    h = h.reshape(B, S, M_WIDTH) * head_norm.astype(jnp.float32)
    return (jax.nn.sigmoid(o_pre.astype(jnp.float32)) * h).astype(q.dtype)


def encoder_layer(x, cos, sin, ffn1_norm, ffn1_w_gate, ffn1_w_up, ffn1_w_down, mix_norm, w_in, q_norm, k_norm,
                  mlstm_gate_bias, mlstm_head_norm, w_up_att, w_up_mlstm, w_out,
                  ffn2_norm, ffn2_w_gate, ffn2_w_up, ffn2_w_down):
    B, S, _ = x.shape
    x = x + 0.5 * swiglu(rms_norm(x, ffn1_norm), ffn1_w_gate, ffn1_w_up, ffn1_w_down)
    h = rms_norm(x, mix_norm)
    proj = h @ w_in
    split_at = [int(i) for i in np.cumsum(IN_SPLITS)[:-1]]
    aq, ak, av, mq, mk, mv, gate_pre, o_pre, g_att, g_mlstm = jnp.split(proj, split_at, axis=-1)
    aq = apply_rope(rms_norm(aq.reshape(B, S, ATT_HEADS, ATT_HEAD_DIM), q_norm), cos, sin)
    ak = apply_rope(rms_norm(ak.reshape(B, S, ATT_KV_HEADS, ATT_HEAD_DIM), k_norm), cos, sin)
    av = av.reshape(B, S, ATT_KV_HEADS, ATT_HEAD_DIM)
    att = gqa_block_attention(aq, ak, av)
    mem = mlstm_bidirectional(mq, mk, mv, gate_pre, mlstm_gate_bias, o_pre, mlstm_head_norm)
    merged = jax.nn.sigmoid(g_att) * (att @ w_up_att) + jax.nn.sigmoid(g_mlstm) * (mem @ w_up_mlstm)
    x = x + merged @ w_out
    x = x + 0.5 * swiglu(rms_norm(x, ffn2_norm), ffn2_w_gate, ffn2_w_up, ffn2_w_down)
    return x


def trunk(x, ffn1_norm, ffn1_w_gate, ffn1_w_up, ffn1_w_down, mix_norm, w_in, q_norm, k_norm,
          mlstm_gate_bias, mlstm_head_norm, w_up_att, w_up_mlstm, w_out,
          ffn2_norm, ffn2_w_gate, ffn2_w_up, ffn2_w_down, final_norm):
    cos, sin = axial_rope_angles(x.shape[1])
    for l in range(DEPTH):
        x = encoder_layer(x, cos, sin, ffn1_norm[l], ffn1_w_gate[l], ffn1_w_up[l], ffn1_w_down[l], mix_norm[l],
                          w_in[l], q_norm[l], k_norm[l], mlstm_gate_bias[l], mlstm_head_norm[l], w_up_att[l],
                          w_up_mlstm[l], w_out[l], ffn2_norm[l], ffn2_w_gate[l], ffn2_w_up[l], ffn2_w_down[l])
    return rms_norm(x, final_norm)


def setup_inputs(seed: int = 0) -> dict:
    key = jax.random.key(seed)
    ks = jax.random.split(key, 24)
    L = DEPTH

    def dense(k, fan_in, shape):
        return jax.random.normal(k, shape, jnp.float32) * (fan_in ** -0.5)

    def gain(k, shape):
        return 1.0 + 0.02 * jax.random.normal(k, shape, jnp.float32)

    b_i = 0.1 * jax.random.normal(ks[20], (L, 2, M_HEADS), jnp.float32)
    b_f = jnp.linspace(FORGET_BIAS_LO, FORGET_BIAS_HI, M_HEADS, dtype=jnp.float32)[None, None, :] \
        + 0.1 * jax.random.normal(ks[21], (L, 2, M_HEADS), jnp.float32)
    gate_bias = jnp.stack([b_i[:, 0], b_f[:, 0], b_i[:, 1], b_f[:, 1]], axis=1).reshape(L, N_GATE_COLS)
    return {
        'x_prompt': jax.random.normal(ks[0], (BATCH, SEQ, D_MODEL), jnp.float32),
        'x_sample': jax.random.normal(ks[1], (DEC_BATCH, DEC_SEQ, D_MODEL), jnp.float32),
        'ffn1_norm': gain(ks[2], (L, D_MODEL)),
        'ffn1_w_gate': dense(ks[3], D_MODEL, (L, D_MODEL, D_FF)),
        'ffn1_w_up': dense(ks[4], D_MODEL, (L, D_MODEL, D_FF)),
        'ffn1_w_down': dense(ks[5], D_FF, (L, D_FF, D_MODEL)),
        'mix_norm': gain(ks[6], (L, D_MODEL)),
        'w_in': dense(ks[7], D_MODEL, (L, D_MODEL, D_IN)),
        'q_norm': gain(ks[8], (L, ATT_HEAD_DIM)),
        'k_norm': gain(ks[9], (L, ATT_HEAD_DIM)),
        'mlstm_gate_bias': gate_bias,
        'mlstm_head_norm': gain(ks[10], (L, M_WIDTH)),
        'w_up_att': dense(ks[11], ATT_WIDTH, (L, ATT_WIDTH, D_MODEL)),
        'w_up_mlstm': dense(ks[12], M_WIDTH, (L, M_WIDTH, D_MODEL)),
        'w_out': dense(ks[13], D_MODEL, (L, D_MODEL, D_MODEL)),
        'ffn2_norm': gain(ks[14], (L, D_MODEL)),
        'ffn2_w_gate': dense(ks[15], D_MODEL, (L, D_MODEL, D_FF)),
        'ffn2_w_up': dense(ks[16], D_MODEL, (L, D_MODEL, D_FF)),
        'ffn2_w_down': dense(ks[17], D_FF, (L, D_FF, D_MODEL)),
        'final_norm': gain(ks[18], (D_MODEL,)),
    }


def reference(x_prompt, x_sample, ffn1_norm, ffn1_w_gate, ffn1_w_up, ffn1_w_down, mix_norm, w_in, q_norm, k_norm,
              mlstm_gate_bias, mlstm_head_norm, w_up_att, w_up_mlstm, w_out,
              ffn2_norm, ffn2_w_gate, ffn2_w_up, ffn2_w_down, final_norm):
    y_prompt = trunk(x_prompt, ffn1_norm, ffn1_w_gate, ffn1_w_up, ffn1_w_down, mix_norm, w_in, q_norm, k_norm,
                     mlstm_gate_bias, mlstm_head_norm, w_up_att, w_up_mlstm, w_out,
                     ffn2_norm, ffn2_w_gate, ffn2_w_up, ffn2_w_down, final_norm)
    y_sample = trunk(x_sample, ffn1_norm, ffn1_w_gate, ffn1_w_up, ffn1_w_down, mix_norm, w_in, q_norm, k_norm,
                     mlstm_gate_bias, mlstm_head_norm, w_up_att, w_up_mlstm, w_out,
                     ffn2_norm, ffn2_w_gate, ffn2_w_up, ffn2_w_down, final_norm)
    return (y_prompt, y_sample)
```

```python
import contextlib
import os
import numpy as np
import concourse.bass as bass
import concourse.mybir as mybir
from concourse.bass_utils import run_bass_kernel_spmd

F32 = mybir.dt.float32
BF16 = mybir.dt.bfloat16
AF = mybir.ActivationFunctionType
ALU = mybir.AluOpType
AX = mybir.AxisListType

D = 1024
S = 2048
NT = S // 128
DFF = 2816
NFC = DFF // 128
EPS = 1e-6
NCORES = 8
SEQ_PER_CORE = 3


class Buf:
    __slots__ = ("w", "r", "name", "excl")

    def __init__(self, name="", excl=False):
        self.w = None
        self.r = {}
        self.name = name
        self.excl = excl


class _Q:
    def __init__(self, name, sem, is_pe=False):
        self.name = name
        self.sem = sem
        self.cnt = 0
        self.ops = []
        self.seen = {}
        self.is_pe = is_pe
        self.dma_sems = []
        self.dma_uses = []
        self.dma_rr = 0


class Prog:
    def __init__(self, nc, stack):
        self.nc = nc
        self.sems = {}
        self.q = {}
        for name, is_pe in (("pe", True), ("act", False), ("dve", False), ("pool", False), ("sp", False)):
            self.sems["s_" + name] = stack.enter_context(nc.semaphore("s_" + name))
            self.q[name] = _Q(name, "s_" + name, is_pe)
        for qn, n in (("sp", 8), ("pool", 16)):
            q = self.q[qn]
            for i in range(n):
                key = "d_%s%d" % (qn, i)
                self.sems[key] = stack.enter_context(nc.semaphore(key))
                q.dma_sems.append(key)
                q.dma_uses.append(0)
        self.n_ops = 0

    def _need(self, q, ev, waits):
        if ev is None:
            return
        key, val = ev
        if q.is_pe and key == q.sem:
            return
        if q.seen.get(key, 0) >= val:
            return
        if waits.get(key, 0) < val:
            waits[key] = val

    def _emit_waits(self, q, r, w, extra=()):
        waits = {}
        for b in r:
            self._need(q, b.w, waits)
            if b.excl:
                for key, val in b.r.items():
                    if key != q.sem:
                        self._need(q, (key, val), waits)
        for b in w:
            self._need(q, b.w, waits)
            for key, val in b.r.items():
                self._need(q, (key, val), waits)
        for ev in extra:
            self._need(q, ev, waits)
        for key, val in waits.items():
            q.seen[key] = val
            sem = self.sems[key]
            q.ops.append(lambda e, sem=sem, val=val: e.wait_ge(sem, val))
            self.n_ops += 1

    def _record(self, ev, r, w):
        key, val = ev
        for b in r:
            if b.r.get(key, 0) < val:
                b.r[key] = val
        for b in w:
            b.w = ev
            b.r = {}

    def op(self, qname, fn, r=(), w=()):
        q = self.q[qname]
        self._emit_waits(q, r, w)
        q.cnt += 1
        sem = self.sems[q.sem]
        q.ops.append(lambda e, fn=fn, sem=sem: fn(e).then_inc(sem, 1))
        self.n_ops += 1
        ev = (q.sem, q.cnt)
        self._record(ev, r, w)
        return ev

    def dma(self, qname, out, in_, r=(), w=()):
        q = self.q[qname]
        i = q.dma_rr
        q.dma_rr = (i + 1) % len(q.dma_sems)
        key = q.dma_sems[i]
        prev = (key, 16 * q.dma_uses[i]) if q.dma_uses[i] else None
        self._emit_waits(q, r, w, extra=(prev,) if prev else ())
        q.dma_uses[i] += 1
        sem = self.sems[key]
        q.ops.append(lambda e, out=out, in_=in_, sem=sem: e.dma_start(out=out, in_=in_).then_inc(sem, 16))
        self.n_ops += 1
        ev = (key, 16 * q.dma_uses[i])
        self._record(ev, r, w)
        return ev

    def all_events(self):
        evs = []
        for q in self.q.values():
            if q.cnt:
                evs.append((q.sem, q.cnt))
            for key, u in zip(q.dma_sems, q.dma_uses):
                if u:
                    evs.append((key, 16 * u))
        return evs

    def barrier(self, qnames=("pe", "act", "dve", "pool", "sp")):
        evs = self.all_events()
        for qn in qnames:
            q = self.q[qn]
            waits = {}
            for ev in evs:
                if ev[0] == q.sem:
                    continue
                self._need(q, ev, waits)
            for key, val in waits.items():
                q.seen[key] = val
                sem = self.sems[key]
                q.ops.append(lambda e, sem=sem, val=val: e.wait_ge(sem, val))

    def finish(self):
        self.barrier(("sp", "pool"))
        nc = self.nc
        with nc.Block() as block:
            @block.tensor
            def _(e):
                for f in self.q["pe"].ops:
                    f(e)

            @block.scalar
            def _(e):
                for f in self.q["act"].ops:
                    f(e)

            @block.vector
            def _(e):
                for f in self.q["dve"].ops:
                    f(e)

            @block.gpsimd
            def _(e):
                for f in self.q["pool"].ops:
                    f(e)

            @block.sync
            def _(e):
                for f in self.q["sp"].ops:
                    f(e)


class Arena:
    def __init__(self, t, words):
        self.t = t
        self.words = words
        self.off = 0
        self.marks = []
        self.peak = 0

    def f32(self, n):
        assert self.off + n <= self.words, ("sbuf arena overflow", self.off, n, self.words)
        ap = self.t[:, self.off:self.off + n]
        self.off += n
        self.peak = max(self.peak, self.off)
        return ap

    def bf16(self, n):
        w = (n + 1) // 2
        assert self.off + w <= self.words, ("sbuf arena overflow", self.off, w, self.words)
        ap = self.t[:, self.off:self.off + w].bitcast(BF16)
        self.off += w
        self.peak = max(self.peak, self.off)
        return ap

    def push(self):
        self.marks.append(self.off)

    def pop(self):
        self.off = self.marks.pop()


class Ring:
    def __init__(self, aps, name):
        self.aps = aps
        self.bufs = [Buf("%s%d" % (name, i)) for i in range(len(aps))]
        self.i = 0

    def next(self):
        i = self.i
        self.i = (i + 1) % len(self.aps)
        return self.aps[i], self.bufs[i]


ARENA_WORDS = 53200


def build_program(nseq=SEQ_PER_CORE, stage="full"):
    noffn = stage.startswith("only_")
    if noffn:
        stage = "mix_" + stage[5:]
    nc = bass.Bass("TRN2", target_bir_lowering=False)

    def din(name, shape):
        return nc.dram_tensor(name, list(shape), F32, kind="ExternalInput").ap()

    x_d = din("x", [nseq, S, D])
    y_d = nc.dram_tensor("y", [nseq, S, D], F32, kind="ExternalOutput").ap()
    w1g_d, w1u_d, w1d_d = din("w1g", [NFC, 128, 1024]), din("w1u", [NFC, 128, 1024]), din("w1d", [DFF, D])
    w2g_d, w2u_d, w2d_d = din("w2g", [NFC, 128, 1024]), din("w2u", [NFC, 128, 1024]), din("w2d", [DFF, D])
    n1_d, nm_d, n2_d, nf_d = din("n1", [D]), din("nm", [D]), din("n2", [D]), din("nf", [D])
    wqkv_d = din("wqkv", [6, 128, 1024])
    wm_d = din("wm", [16, 128, 1024])
    wmg_d = din("wmg", [128, 128])
    wg_d = din("wg", [16, 128, 1024])
    wua_d, wum_d, wout_d = din("wua", [8, 128, 512]), din("wum", [8, 128, 512]), din("wout", [D, D])
    gq_d, gk_d, gb_d, hn_d = din("gq", [64]), din("gk", [64]), din("gb", [16]), din("hn", [512])
    ident_d = din("ident", [128, 128])
    cos_d, sin_d = din("cos", [S, 32]), din("sin", [S, 32])
    tri_d = din("tri", [3, 128, 128])
    msk_d = din("msk", [2, 128, 128])

    with contextlib.ExitStack() as st:
        P = Prog(nc, st)
        arena_t = st.enter_context(nc.sbuf_tensor("arena", [128, ARENA_WORDS], F32))
        A = Arena(arena_t, ARENA_WORDS)
        psall = st.enter_context(nc.psum_tensor("psall", [128, 4096], F32))
        ps = [psall[:, i * 512:(i + 1) * 512] for i in range(8)]
        pb = [Buf("ps%d" % i, excl=True) for i in range(8)]

        def PE_MM(out, lhsT, rhs, start, stop, r, w):
            P.op("pe", lambda e: e.matmul(out, lhsT=lhsT, rhs=rhs, start=start, stop=stop), r, w)

        def PE_T(out, in_, ident, r, w):
            P.op("pe", lambda e: e.transpose(out=out, in_=in_, identity=ident), r, w)

        def ACT(r, w, **kw):
            P.op("act", lambda e: e.activation(**kw), r, w)

        def OP(qn, name, r, w, **kw):
            P.op(qn, lambda e: getattr(e, name)(**kw), r, w)

        X = A.f32(NT * D)
        Xv = X.rearrange("p (t d) -> p t d", t=NT)
        bX = [Buf("X%d" % i) for i in range(NT)]
        identf = A.f32(128); b_identf = Buf()
        identb = A.bf16(128); b_identb = Buf()
        cos_t = A.f32(NT * 32); sin_t = A.f32(NT * 32); b_cs = Buf()
        tri = A.f32(3 * 128); b_tri = Buf()
        triv = tri.rearrange("p (a b) -> p a b", a=3)
        msk = A.f32(2 * 128); b_msk = Buf()
        mskv = msk.rearrange("p (a b) -> p a b", a=2)
        gbc = A.f32(D); b_gbc = Buf()
        gq_bc = A.f32(64); gk_bc = A.f32(64); gb_bc = A.f32(16); hn_bc = A.f32(512); b_small = Buf()
        gfin = A.f32(D); b_gfin = Buf()
        yo_ring = Ring([A.f32(D) for _ in range(2)], "yo")
        rstd_all = A.f32(NT); b_rstd = [Buf() for _ in range(NT)]
        ss_all = A.f32(NT); b_ss = [Buf() for _ in range(NT)]
        ones_bf = A.bf16(2); b_ones = Buf()
        junk = A.bf16(D); b_junk = Buf()

        scr_blk = nc.dram_tensor("scr_blk", [2, NT, 128, 1536], BF16).ap()
        scr_vo = nc.dram_tensor("scr_vo", [NT, 128, 1024], BF16).ap()
        b_scr_blk = [[Buf() for _ in range(NT)] for _ in range(2)]
        b_scr_vo = [Buf() for _ in range(NT)]
        dec_all = A.f32(2 * NT * 4); dec_allv = dec_all.rearrange("p (d t h) -> p d t h", d=2, t=NT)
        b_dec = [[Buf() for _ in range(NT)] for _ in range(2)]
        P.dma("sp", identf, ident_d[:, :], w=[b_identf])
        OP("dve", "tensor_copy", [b_identf], [b_identb], out=identb, in_=identf)
        P.dma("sp", cos_t.rearrange("p (t j) -> p t j", t=NT), cos_d.rearrange("(t p) j -> p t j", p=128), w=[b_cs])
        P.dma("sp", sin_t.rearrange("p (t j) -> p t j", t=NT), sin_d.rearrange("(t p) j -> p t j", p=128), w=[b_cs])
        P.dma("sp", triv, tri_d.rearrange("a p b -> p a b"), w=[b_tri])
        P.dma("sp", mskv, msk_d.rearrange("a p b -> p a b"), w=[b_msk])
        P.dma("sp", gq_bc, gq_d.partition_broadcast(128), w=[b_small])
        P.dma("sp", gk_bc, gk_d.partition_broadcast(128), w=[b_small])
        P.dma("sp", gb_bc, gb_d.partition_broadcast(128), w=[b_small])
        P.dma("sp", hn_bc, hn_d.partition_broadcast(128), w=[b_small])
        OP("dve", "memset", [], [b_ones], ap=ones_bf, constant=1.0)
        P.dma("sp", gfin, nf_d.partition_broadcast(128), w=[b_gfin])
        gqk_bc = A.f32(640); b_gqk = Buf()
        gqkv = gqk_bc.rearrange("p (h d) -> p h d", h=10)
        OP("dve", "tensor_copy", [b_small], [b_gqk], out=gqkv[:, 0:8, :], in_=gq_bc.unsqueeze(1).broadcast_to([128, 8, 64]))
        OP("dve", "tensor_copy", [b_small, b_gqk], [b_gqk], out=gqkv[:, 8:10, :], in_=gk_bc.unsqueeze(1).broadcast_to([128, 2, 64]))
        OP("dve", "memset", [], b_ss, ap=ss_all, constant=0.0)
        cosv = cos_t.rearrange("p (t j) -> p t j", t=NT)
        sinv = sin_t.rearrange("p (t j) -> p t j", t=NT)

        def load_cast(dram_ap, shape, dst, r_dst, w_dst):
            P.dma("pool", dst, dram_ap, r=list(r_dst), w=list(w_dst))

        def colblock(w_d, c0, ncols):
            return w_d[:, c0:c0 + ncols].rearrange("(kc p) n -> p kc n", p=128)

        def load_gain(g_d):
            P.dma("sp", gbc, g_d.partition_broadcast(128), w=[b_gbc])

        def ss_square(tt):
            ACT([bX[tt], b_ss[tt]], [b_junk, b_ss[tt]], out=junk, in_=Xv[:, tt, :], func=AF.Square,
                accum_out=ss_all[:, tt:tt + 1])

        def rstd_finish():
            ACT(b_ss, b_rstd, out=rstd_all, in_=ss_all, func=AF.Sqrt, scale=1.0 / D, bias=EPS)
            OP("dve", "reciprocal", b_rstd, b_rstd, out=rstd_all, in_=rstd_all)
            OP("dve", "memset", [], b_ss, ap=ss_all, constant=0.0)

        def norm_transpose(tt, xn_ring, tp_bank, dst, b_dst, evac="act"):
            xn, bxn = xn_ring.next()
            OP("dve", "scalar_tensor_tensor", [bX[tt], b_rstd[tt], b_gbc], [bxn], out=xn, in0=Xv[:, tt, :],
               scalar=rstd_all[:, tt:tt + 1], in1=gbc, op0=ALU.mult, op1=ALU.mult)
            pT = ps[tp_bank][:, :].bitcast(BF16)
            for kc in range(8):
                PE_T(pT[:, kc * 128:(kc + 1) * 128], xn[:, kc * 128:(kc + 1) * 128], identb,
                     [bxn, b_identb], [pb[tp_bank]])
            if evac == "act":
                ACT([pb[tp_bank]], [b_dst], out=dst, in_=pT.rearrange("p (k t) -> p k t", k=8), func=AF.Copy)
            else:
                OP("dve", "tensor_copy", [pb[tp_bank]], [b_dst], out=dst,
                   in_=pT.rearrange("p (k t) -> p k t", k=8))

        def ffn(g_d, wg_dd, wu_dd, wd_dd, final_cb=None):
            A.push()
            load_gain(g_d)
            NG, CG = 2, NFC // 2
            xnT = A.bf16(8 * 1024); xnTv = xnT.rearrange("p (k t) -> p k t", k=8)
            b_xnT = [Buf() for _ in range(8)]
            hT = A.bf16(CG * 1024); hTv = hT.rearrange("p (c t) -> p c t", c=CG)
            b_hT = [[Buf() for _ in range(2)] for _ in range(CG)]
            WdR = [A.bf16(CG * D) for _ in range(2)]
            WdRv = [w.rearrange("p (c n) -> p c n", c=CG) for w in WdR]
            b_Wd = [[Buf() for _ in range(CG)] for _ in range(2)]
            wg_ring = Ring([A.bf16(1024) for _ in range(4)], "wg")
            wu_ring = Ring([A.bf16(1024) for _ in range(4)], "wu")
            xn_ring = Ring([A.bf16(D) for _ in range(2)], "xn")
            sg_ring = Ring([A.f32(512) for _ in range(2)], "sg")
            rstd_finish()
            it = 0
            for sbk in range(2):
                for i in range(8):
                    norm_transpose(sbk * 8 + i, xn_ring, 6 + (i % 2), xnTv[:, :, i * 128:(i + 1) * 128], b_xnT[i])
                for g in range(NG):
                    wsel = (sbk * NG + g) % 2
                    for cl in range(CG):
                        c = g * CG + cl
                        wg, bwg = wg_ring.next()
                        wu, bwu = wu_ring.next()
                        wgv = wg.rearrange("p (k n) -> p k n", k=8)
                        wuv = wu.rearrange("p (k n) -> p k n", k=8)
                        load_cast(wg_dd[c], [128, 1024], wg, [], [bwg])
                        load_cast(wu_dd[c], [128, 1024], wu, [], [bwu])
                        load_cast(wd_dd[c * 128:(c + 1) * 128, :], [128, D], WdRv[wsel][:, cl, :], [], [b_Wd[wsel][cl]])
                        for th in range(2):
                            gb_, ub_ = it % 2, 2 + it % 2
                            it += 1
                            tsl = slice(th * 512, (th + 1) * 512)
                            for kc in range(8):
                                PE_MM(ps[gb_][:, :], wgv[:, kc, :], xnTv[:, kc, tsl], kc == 0, kc == 7,
                                      [bwg] + b_xnT[th * 4:th * 4 + 4], [pb[gb_]])
                            for kc in range(8):
                                PE_MM(ps[ub_][:, :], wuv[:, kc, :], xnTv[:, kc, tsl], kc == 0, kc == 7,
                                      [bwu] + b_xnT[th * 4:th * 4 + 4], [pb[ub_]])
                            sg, bsg = sg_ring.next()
                            ACT([pb[gb_]], [bsg], out=sg, in_=ps[gb_][:, :], func=AF.Silu)
                            OP("dve", "tensor_tensor", [bsg, pb[ub_]], [b_hT[cl][th]], out=hTv[:, cl, tsl], in0=sg,
                               in1=ps[ub_][:, :], op=ALU.mult)
                    for i in range(8):
                        tt = sbk * 8 + i
                        for half in range(2):
                            yb = 4 + half
                            for cl in range(CG):
                                PE_MM(ps[yb][:, :], hTv[:, cl, i * 128:(i + 1) * 128],
                                      WdRv[wsel][:, cl, half * 512:(half + 1) * 512],
                                      cl == 0, cl == CG - 1, [b_hT[cl][i // 4], b_Wd[wsel][cl]], [pb[yb]])
                            xs = Xv[:, tt, half * 512:(half + 1) * 512]
                            OP("dve", "scalar_tensor_tensor", [pb[yb], bX[tt]], [bX[tt]], out=xs, in0=ps[yb][:, :],
                               scalar=0.5, in1=xs, op0=ALU.mult, op1=ALU.add)
                            if g == NG - 1 and half == 1:
                                ss_square(tt)
                                if final_cb is not None:
                                    final_cb(tt)
            P.barrier()
            A.pop()

        def mixer():
            A.push()
            load_gain(nm_d)
            rstd_finish()
            memT = A.bf16(4 * S); memTv = memT.rearrange("p (h t) -> p h t", h=4)
            b_memT = [Buf() for _ in range(NT)]
            xn_ring = Ring([A.bf16(D) for _ in range(2)], "xn")
            Wq = A.bf16(6 * 1024); Wqb = Wq.rearrange("p (c k n) -> p c k n", c=6, k=8); b_WqB = [Buf() for _ in range(6)]

            def h4(ap):
                return ap.rearrange("p (h e) -> p h e", h=4)

            A.push()
            if stage == "mix_b":
                OP("dve", "memset", [], b_memT, ap=memT, constant=0.0)
            Wm = A.bf16(16 * 1024); Wmb = Wm.rearrange("p (c k n) -> p c k n", c=16, k=8)
            Wmg = A.bf16(128); Wmgv = Wmg.rearrange("p (k n) -> p k n", k=8)
            b_WmB = [Buf() for _ in range(17)]
            b_WmJ = [b_WmB[4 * j_:4 * j_ + 4] for j_ in range(4)] + [[b_WmB[16]]]
            for cb in range(16):
                load_cast(wm_d[cb], [128, 1024], Wm[:, cb * 1024:(cb + 1) * 1024], [], [b_WmB[cb]])
                if cb == 7:
                    load_cast(wmg_d, [128, 128], Wmg, [], [b_WmB[16]])
            for cb in range(6):
                load_cast(wqkv_d[cb], [128, 1024], Wq[:, cb * 1024:(cb + 1) * 1024], [], [b_WqB[cb]])
            hTt_ring = Ring([A.bf16(1024) for _ in range(3)], "hTt")
            blk_ring = [Ring([A.bf16(1536) for _ in range(2)], "blk%d" % d_) for d_ in range(2)]
            vo_ring = Ring([A.bf16(1024) for _ in range(2)], "vo")
            G = A.f32(16); b_G = Buf()
            sp8 = A.f32(8); b_sp8 = Buf()
            cums = A.f32(16); b_cums = Buf()
            agb = A.f32(24); b_agb = Buf()
            tmp8 = A.f32(8); b_tmp8 = Buf()
            QaK = [[A.bf16(512) for _ in range(2)] for _ in range(2)]
            b_QaK = [[Buf() for _ in range(2)] for _ in range(2)]
            smallp = ps[6]
            pp = {}

            hts = {}

            def ppNT(tt):
                hTt, b_hTt = hTt_ring.next()
                hTtv = hTt.rearrange("p (k t) -> p k t", k=8)
                hts[tt] = (hTtv, b_hTt)
                norm_transpose(tt, xn_ring, 7, hTtv, b_hTt, evac="dve")

            def ppA1(tt):
                hTtv, b_hTt = hts[tt]
                qb_, kb_ = 2 * (tt % 2), 2 * (tt % 2) + 1
                for j, bank in ((0, qb_), (1, kb_)):
                    for kc in range(8):
                        PE_MM(ps[bank], hTtv[:, kc, :], Wmb[:, 4 * j:4 * j + 4, kc, :], kc == 0, kc == 7,
                              [b_hTt] + b_WmJ[j], [pb[bank]])

            def ppA2(tt):
                hTtv, b_hTt = hts.pop(tt)
                for j, bank in ((2, 4), (3, 5)):
                    for kc in range(8):
                        PE_MM(ps[bank], hTtv[:, kc, :], Wmb[:, 4 * j:4 * j + 4, kc, :], kc == 0, kc == 7,
                              [b_hTt] + b_WmJ[j], [pb[bank]])
                for kc in range(8):
                    PE_MM(smallp[:, 0:16], hTtv[:, kc, :], Wmgv[:, kc, :], kc == 0, kc == 7,
                          [b_hTt] + b_WmJ[4], [pb[6]])

            vos = {}

            def ppBg(tt):
                OP("dve", "tensor_tensor", [pb[6], b_small], [b_G], out=G, in0=smallp[:, 0:16], in1=gb_bc, op=ALU.add)
                ACT([b_G], [b_sp8], out=sp8, in_=G[:, 8:16], func=AF.Exp, scale=-1.0)
                ACT([b_sp8], [b_sp8], out=sp8, in_=sp8, func=AF.Ln, bias=1.0)
                for d_ in range(2):
                    PE_MM(smallp[:, 16 + 4 * d_:20 + 4 * d_], triv[:, d_, :], sp8[:, 4 * d_:4 * d_ + 4], True, True,
                          [b_tri, b_sp8], [pb[6]])
                PE_MM(smallp[:, 24:32], triv[:, 2, :], sp8, True, True, [b_tri, b_sp8], [pb[6]])
                OP("dve", "tensor_copy", [pb[6]], [b_cums], out=cums, in_=smallp[:, 16:32])
                ACT([b_cums], [b_agb], out=agb[:, 0:8], in_=cums[:, 0:8], func=AF.Exp, scale=-1.0)
                OP("dve", "tensor_tensor", [b_G, b_cums], [b_tmp8], out=tmp8, in0=G[:, 0:8], in1=cums[:, 0:8], op=ALU.add)
                ACT([b_tmp8], [b_agb], out=agb[:, 8:16], in_=tmp8, func=AF.Exp)
                OP("dve", "tensor_tensor", [b_tmp8, b_cums], [b_tmp8], out=tmp8, in0=tmp8, in1=cums[:, 8:16], op=ALU.subtract)
                ACT([b_tmp8], [b_agb], out=agb[:, 16:24], in_=tmp8, func=AF.Exp)
                for d_ in range(2):
                    ACT([b_cums], [b_dec[d_][tt]], out=dec_allv[:, d_, tt, :], in_=cums[:, 8 + 4 * d_:12 + 4 * d_], func=AF.Exp, scale=-1.0)

            def ppB(tt):
                qb_, kb_ = 2 * (tt % 2), 2 * (tt % 2) + 1
                vo, b_vo = vo_ring.next()
                ACT([pb[4]], [b_vo], out=vo[:, 0:512], in_=ps[4], func=AF.Copy)
                ACT([pb[5]], [b_vo], out=vo[:, 512:1024], in_=ps[5], func=AF.Sigmoid)
                blks = []
                for d_ in range(2):
                    blk, b_blk = blk_ring[d_].next()
                    blks.append((blk, b_blk))
                    al = agb[:, 4 * d_:4 * d_ + 4]; gm = agb[:, 8 + 4 * d_:12 + 4 * d_]; be = agb[:, 16 + 4 * d_:20 + 4 * d_]
                    OP("dve", "tensor_tensor", [pb[qb_], b_agb], [b_QaK[d_][0]], out=h4(QaK[d_][0]), in0=h4(ps[qb_]),
                       in1=al.unsqueeze(2).broadcast_to([128, 4, 128]), op=ALU.mult)
                    OP("dve", "scalar_tensor_tensor", [pb[kb_], b_agb], [b_QaK[d_][1]], out=h4(QaK[d_][1]), in0=h4(ps[kb_]),
                       scalar=128.0 ** -0.5, in1=gm.unsqueeze(2).broadcast_to([128, 4, 128]), op0=ALU.mult, op1=ALU.mult)
                    OP("dve", "scalar_tensor_tensor", [pb[kb_], b_agb], [b_blk], out=h4(blk[:, 1024:1536]), in0=h4(ps[kb_]),
                       scalar=128.0 ** -0.5, in1=be.unsqueeze(2).broadcast_to([128, 4, 128]), op0=ALU.mult, op1=ALU.mult)
                pp[tt] = (vo, b_vo, blks)

            def ppC(tt):
                vo, b_vo, blks = pp.pop(tt)
                pT = ps[7].bitcast(BF16)
                for d_ in range(2):
                    blk, b_blk = blks[d_]
                    for h in range(4):
                        PE_T(pT[:, h * 128:(h + 1) * 128], QaK[d_][0][:, h * 128:(h + 1) * 128], identb, [b_QaK[d_][0], b_identb], [pb[7]])
                    for h in range(4):
                        PE_T(pT[:, 512 + h * 128:512 + (h + 1) * 128], QaK[d_][1][:, h * 128:(h + 1) * 128], identb, [b_QaK[d_][1], b_identb], [pb[7]])
                    OP("dve", "tensor_copy", [pb[7]], [b_blk], out=blk[:, 0:1024], in_=pT[:, 0:1024])
                    P.dma("sp", scr_blk[d_, tt], blk, r=[b_blk], w=[b_scr_blk[d_][tt]])
                P.dma("sp", scr_vo[tt], vo, r=[b_vo], w=[b_scr_vo[tt]])

            if stage != "mix_b":
                ppNT(0)
                ppNT(1)
                ppA1(0)
                ppA2(0)
                for tt in range(NT):
                    if tt + 1 < NT:
                        ppA1(tt + 1)
                    ppBg(tt)
                    if tt + 2 < NT:
                        ppNT(tt + 2)
                    ppB(tt)
                    if tt + 1 < NT:
                        ppA2(tt + 1)
                    ppC(tt)
            P.barrier()
            A.pop()

            A.push()
            Hf = A.f32(NT * 512); Hfv = Hf.rearrange("p (t h e) -> p t h e", t=NT, h=4)
            b_Hf = [Buf() for _ in range(NT)]
            small = ps[3]
            NSL = 3

            def mk_dir():
                dd = {}
                dd["C"] = A.f32(512); dd["bC"] = Buf()
                dd["Cb"] = A.bf16(512); dd["bCb"] = Buf()
                dd["n"] = A.f32(4); dd["bn"] = Buf()
                dd["nb"] = A.bf16(4); dd["bnb"] = Buf()
                dd["SmT"] = A.bf16(512); dd["bSmT"] = Buf()
                dd["rden"] = A.f32(4); dd["brden"] = Buf()
                dd["slots"] = []
                for _ in range(NSL):
                    so = {"blk": A.bf16(1536), "bblk": Buf(), "vo": A.bf16(1024), "bvo": Buf()}
                    dd["slots"].append(so)
                return dd

            dirs = [mk_dir(), mk_dir()]
            sq2 = A.f32(512); b_sq2 = Buf()
            dirs[0]["hsc"] = sq2; dirs[0]["bhsc"] = b_sq2
            dirs[1]["hsc"] = A.f32(512); dirs[1]["bhsc"] = Buf()
            ssh = A.f32(4); b_ssh = Buf()
            hs2 = sq2; b_hs2 = b_sq2
            memt = A.bf16(512); b_memt = Buf()

            def fetch(dirn, tt, slot):
                so = dirs[dirn]["slots"][slot]
                P.dma("sp", so["blk"], scr_blk[dirn, tt], r=[b_scr_blk[dirn][tt]], w=[so["bblk"]])
                P.dma("sp", so["vo"], scr_vo[tt], r=[b_scr_vo[tt]], w=[so["bvo"]])

            dbanks = [(5, 6, 7, 3, 32), (0, 1, 2, 4, 256)]

            def rec_pe1(dirn, slot):
                dd = dirs[dirn]; so = dd["slots"][slot]
                bST = dbanks[dirn][0]
                QaTv, KgTv = h4(so["blk"][:, 0:512]), h4(so["blk"][:, 512:1024])
                for h in range(4):
                    PE_MM(ps[bST][:, h * 128:(h + 1) * 128], KgTv[:, h, :], QaTv[:, h, :], True, True,
                          [so["bblk"]], [pb[bST]])
                OP("dve", "tensor_tensor", [pb[bST], b_msk], [dd["bSmT"]], out=h4(dd["SmT"]), in0=h4(ps[bST]),
                   in1=mskv[:, dirn, :].unsqueeze(1).broadcast_to([128, 4, 128]), op=ALU.mult)

            def rec_pe2(dirn, slot, tt, first):
                dd = dirs[dirn]; so = dd["slots"][slot]
                _, bNUM, bDC, bSM, dn0 = dbanks[dirn]
                smallb = ps[bSM]
                QaTv, Kbv, Vtv = h4(so["blk"][:, 0:512]), h4(so["blk"][:, 1024:1536]), h4(so["vo"][:, 0:512])
                SmTv = h4(dd["SmT"])
                Cbv = h4(dd["Cb"]); Cstv = h4(dd["C"])
                nb, nst, rden = dd["nb"], dd["n"], dd["rden"]
                for h in range(4):
                    PE_MM(ps[bDC][:, h * 128:(h + 1) * 128], Kbv[:, h, :], Vtv[:, h, :], True, True,
                          [so["bblk"], so["bvo"]], [pb[bDC]])
                    PE_MM(smallb[:, dn0 + 4 + h:dn0 + 5 + h], Kbv[:, h, :], ones_bf[:, 0:1], True, True,
                          [so["bblk"], b_ones], [pb[bSM]])
                for h in range(4):
                    PE_MM(ps[bNUM][:, h * 128:(h + 1) * 128], SmTv[:, h, :], Vtv[:, h, :], True, False,
                          [dd["bSmT"], so["bvo"]], [pb[bNUM]])
                    PE_MM(ps[bNUM][:, h * 128:(h + 1) * 128], QaTv[:, h, :], Cbv[:, h, :], False, True,
                          [so["bblk"], dd["bCb"]], [pb[bNUM]])
                    PE_MM(smallb[:, dn0 + h:dn0 + h + 1], SmTv[:, h, :], ones_bf[:, 0:1], True, False,
                          [dd["bSmT"], b_ones], [pb[bSM]])
                    PE_MM(smallb[:, dn0 + h:dn0 + h + 1], QaTv[:, h, :], nb[:, h:h + 1], False, True,
                          [so["bblk"], dd["bnb"]], [pb[bSM]])
                dec = dec_allv[:, dirn, tt, :]
                OP("pool", "tensor_tensor", [dd["bC"], b_dec[dirn][tt], dd["bCb"]], [dd["bC"]], out=Cstv, in0=Cstv,
                   in1=dec.unsqueeze(2).broadcast_to([128, 4, 128]), op=ALU.mult)
                ACT([pb[bSM]], [dd["brden"]], out=rden, in_=smallb[:, dn0:dn0 + 4], func=AF.Abs)
                OP("dve", "tensor_scalar_max", [dd["brden"]], [dd["brden"]], out=rden, in0=rden, scalar1=1.0)
                OP("dve", "reciprocal", [dd["brden"]], [dd["brden"]], out=rden, in_=rden)
                numv = h4(ps[bNUM])
                rb = rden.unsqueeze(2).broadcast_to([128, 4, 128])
                if first:
                    OP("dve", "tensor_tensor", [pb[bNUM], dd["brden"]], [b_Hf[tt]], out=Hfv[:, tt], in0=numv, in1=rb, op=ALU.mult)
                else:
                    hscv = h4(dd["hsc"])
                    OP("dve", "tensor_tensor", [pb[bNUM], dd["brden"]], [dd["bhsc"]], out=hscv, in0=numv, in1=rb, op=ALU.mult)
                    OP("pool", "tensor_tensor", [dd["bhsc"], b_Hf[tt]], [b_Hf[tt]], out=Hfv[:, tt], in0=Hfv[:, tt],
                       in1=hscv, op=ALU.add)
                OP("dve", "tensor_tensor", [dd["bC"], pb[bDC]], [dd["bC"]], out=dd["C"], in0=dd["C"], in1=ps[bDC], op=ALU.add)
                OP("dve", "tensor_tensor", [dd["bn"], b_dec[dirn][tt]], [dd["bn"]], out=nst, in0=nst, in1=dec, op=ALU.mult)
                OP("dve", "tensor_tensor", [dd["bn"], pb[bSM]], [dd["bn"]], out=nst, in0=nst, in1=smallb[:, dn0 + 4:dn0 + 8], op=ALU.add)
                ACT([dd["bC"]], [dd["bCb"]], out=dd["Cb"], in_=dd["C"], func=AF.Copy)
                OP("dve", "tensor_copy", [dd["bn"]], [dd["bnb"]], out=nb, in_=nst)

            def finalize(dirn, slot, tt):
                dd = dirs[dirn]; so = dd["slots"][slot]
                ACT([b_Hf[tt]], [b_sq2], out=sq2, in_=Hf[:, tt * 512:(tt + 1) * 512], func=AF.Square)
                OP("dve", "tensor_reduce", [b_sq2], [b_ssh], out=ssh, in_=h4(sq2), axis=AX.X, op=ALU.add)
                ACT([b_ssh], [b_ssh], out=ssh, in_=ssh, func=AF.Sqrt, scale=1.0 / 128, bias=EPS)
                OP("dve", "reciprocal", [b_ssh], [b_ssh], out=ssh, in_=ssh)
                OP("dve", "tensor_tensor", [b_Hf[tt], b_ssh], [b_hs2], out=h4(hs2), in0=Hfv[:, tt],
                   in1=ssh.unsqueeze(2).broadcast_to([128, 4, 128]), op=ALU.mult)
                OP("pool", "tensor_tensor", [b_hs2, b_small], [b_hs2], out=hs2, in0=hs2, in1=hn_bc, op=ALU.mult)
                OP("dve", "tensor_tensor", [b_hs2, so["bvo"]], [b_memt], out=memt, in0=hs2, in1=so["vo"][:, 512:1024], op=ALU.mult)
                pT2 = ps[4].bitcast(BF16)
                for h in range(4):
                    PE_T(pT2[:, h * 128:(h + 1) * 128], memt[:, h * 128:(h + 1) * 128], identb, [b_memt, b_identb], [pb[4]])
                OP("dve", "tensor_copy", [pb[4]], [b_memT[tt]], out=memTv[:, :, tt * 128:(tt + 1) * 128],
                   in_=pT2[:, 0:512].rearrange("p (h t) -> p h t", h=4))

            if stage != "mix_b":
                for dd in dirs:
                    OP("dve", "memset", [], [dd["bC"]], ap=dd["C"], constant=0.0)
                    OP("dve", "memset", [], [dd["bCb"]], ap=dd["Cb"], constant=0.0)
                    OP("dve", "memset", [], [dd["bn"]], ap=dd["n"], constant=0.0)
                    OP("dve", "memset", [], [dd["bnb"]], ap=dd["nb"], constant=0.0)
                tile_of = lambda dirn, i: i if dirn == 0 else NT - 1 - i
                for pre in range(NSL - 1):
                    fetch(0, tile_of(0, pre), pre % NSL)
                    fetch(1, tile_of(1, pre), pre % NSL)
                for i in range(NT):
                    slot = i % NSL
                    nx = i + NSL - 1
                    if nx < NT:
                        fetch(0, tile_of(0, nx), nx % NSL)
                        fetch(1, tile_of(1, nx), nx % NSL)
                    first = i < NT // 2
                    tf, tb_ = tile_of(0, i), tile_of(1, i)
                    rec_pe1(0, slot)
                    rec_pe1(1, slot)
                    rec_pe2(0, slot, tf, first)
                    rec_pe2(1, slot, tb_, first)
                    if not first:
                        finalize(0, slot, tf)
                        finalize(1, slot, tb_)
            P.barrier()
            A.pop()

            if stage == "mix_a":
                OP("dve", "tensor_copy", b_memT + bX, bX, out=X[:, 0:4 * S], in_=memT)
                P.barrier()
                A.pop()
                return
            attT = A.bf16(4 * S); attTv = attT.rearrange("p (h t) -> p h t", h=4)
            b_attT = [Buf() for _ in range(4)]
            A.push()
            QT = A.bf16(4 * S); QTv = QT.rearrange("p (j t) -> p j t", j=4); b_QT = [Buf() for _ in range(NT)]
            KT = A.bf16(S); b_KT = [Buf() for _ in range(NT)]
            Va = A.bf16(NT * 256); Vav = Va.rearrange("p (t k e) -> p t k e", t=NT, k=2); b_Va = [Buf() for _ in range(NT)]
            OP("dve", "memset", [], b_Va, ap=Va, constant=1.0)
            NBT = 2
            hTt_ring = Ring([A.bf16(1024) for _ in range(2 * NBT)], "hTt")
            sqq_r = Ring([A.f32(640) for _ in range(NBT)], "sqq")
            ssq_r = Ring([A.f32(10) for _ in range(NBT)], "ssq")
            qn_r = Ring([A.f32(640) for _ in range(NBT)], "qn")
            t1_r = Ring([A.f32(320) for _ in range(NBT)], "t1")
            t2_r = Ring([A.f32(320) for _ in range(NBT)], "t2")
            qr_r = Ring([A.bf16(640) for _ in range(2 * NBT)], "qr")
            Rr = A.f32(512); b_Rr = Buf()
            tstate = {}

            def kvslot(tt):
                return 4 + ((tt // 2) % 2), (tt % 2) * 256

            hta = {}

            def stA_nt(tt):
                hTt, b_hTt = hTt_ring.next()
                hTtv = hTt.rearrange("p (k t) -> p k t", k=8)
                hta[tt] = (hTtv, b_hTt)
                norm_transpose(tt, xn_ring, 6 + (tt % 2), hTtv, b_hTt, evac="dve")

            def stA_mm(tt):
                hTtv, b_hTt = hta.pop(tt)
                qb_ = tt % 4
                kvb, kvo = kvslot(tt)
                for kc in range(8):
                    PE_MM(ps[qb_], hTtv[:, kc, :], Wqb[:, 0:4, kc, :], kc == 0, kc == 7, [b_hTt] + b_WqB[0:4], [pb[qb_]])
                for kc in range(8):
                    PE_MM(ps[kvb][:, kvo:kvo + 256], hTtv[:, kc, :], Wqb[:, 4:6, kc, :], kc == 0, kc == 7, [b_hTt] + b_WqB[4:6], [pb[kvb]])

            def stB(tts):
                L = []
                for tt in tts:
                    d_ = {"tt": tt, "qb": tt % 4}
                    d_["kvb"], d_["kvo"] = kvslot(tt)
                    for nm, rg in (("sqq", sqq_r), ("ssq", ssq_r), ("qn", qn_r), ("t1", t1_r), ("t2", t2_r), ("qr", qr_r)):
                        d_[nm], d_["b" + nm] = rg.next()
                    L.append(d_)

                def each(fn):
                    for d_ in L:
                        fn(d_)
                each(lambda d: ACT([pb[d["kvb"]]], [b_Va[d["tt"]]], out=Vav[:, d["tt"], :, 0:64],
                                   in_=ps[d["kvb"]][:, d["kvo"] + 128:d["kvo"] + 256].rearrange("p (k e) -> p k e", k=2), func=AF.Copy))
                each(lambda d: ACT([pb[d["qb"]]], [d["bsqq"]], out=d["sqq"][:, 0:512], in_=ps[d["qb"]], func=AF.Square))
                each(lambda d: ACT([pb[d["kvb"]]], [d["bsqq"]], out=d["sqq"][:, 512:640], in_=ps[d["kvb"]][:, d["kvo"]:d["kvo"] + 128], func=AF.Square))
                each(lambda d: OP("dve", "tensor_reduce", [d["bsqq"]], [d["bssq"]], out=d["ssq"], in_=d["sqq"].rearrange("p (h d) -> p h d", h=10), axis=AX.X, op=ALU.add))
                each(lambda d: ACT([d["bssq"]], [d["bssq"]], out=d["ssq"], in_=d["ssq"], func=AF.Sqrt, scale=1.0 / 64, bias=EPS))
                each(lambda d: OP("dve", "reciprocal", [d["bssq"]], [d["bssq"]], out=d["ssq"], in_=d["ssq"]))
                each(lambda d: OP("dve", "tensor_tensor", [pb[d["qb"]], d["bssq"]], [d["bqn"]], out=d["qn"].rearrange("p (h d) -> p h d", h=10)[:, 0:8, :],
                                  in0=ps[d["qb"]].rearrange("p (h d) -> p h d", h=8),
                                  in1=d["ssq"][:, 0:8].unsqueeze(2).broadcast_to([128, 8, 64]), op=ALU.mult))
                each(lambda d: OP("dve", "tensor_tensor", [pb[d["kvb"]], d["bssq"]], [d["bqn"]], out=d["qn"].rearrange("p (h d) -> p h d", h=10)[:, 8:10, :],
                                  in0=ps[d["kvb"]][:, d["kvo"]:d["kvo"] + 128].rearrange("p (h d) -> p h d", h=2),
                                  in1=d["ssq"][:, 8:10].unsqueeze(2).broadcast_to([128, 2, 64]), op=ALU.mult))
                each(lambda d: OP("dve", "tensor_tensor", [d["bqn"], b_gqk], [d["bqn"]], out=d["qn"], in0=d["qn"], in1=gqk_bc, op=ALU.mult))

                def views(d):
                    q4 = d["qn"].rearrange("p (h j two) -> p h j two", h=10, two=2)
                    cb_ = cosv[:, d["tt"], :].unsqueeze(1).broadcast_to([128, 10, 32])
                    sb_ = sinv[:, d["tt"], :].unsqueeze(1).broadcast_to([128, 10, 32])
                    t1v = d["t1"].rearrange("p (h j) -> p h j", h=10); t2v = d["t2"].rearrange("p (h j) -> p h j", h=10)
                    qr4 = d["qr"].rearrange("p (h j two) -> p h j two", h=10, two=2)
                    return q4[:, :, :, 0], q4[:, :, :, 1], cb_, sb_, t1v, t2v, qr4
                each(lambda d: OP("dve", "tensor_tensor", [d["bqn"], b_cs], [d["bt1"]], out=views(d)[4], in0=views(d)[0], in1=views(d)[2], op=ALU.mult))
                each(lambda d: OP("pool", "tensor_tensor", [d["bqn"], b_cs], [d["bt2"]], out=views(d)[5], in0=views(d)[1], in1=views(d)[3], op=ALU.mult))
                each(lambda d: OP("dve", "tensor_tensor", [d["bt1"], d["bt2"]], [d["bqr"]], out=views(d)[6][:, :, :, 0], in0=views(d)[4], in1=views(d)[5], op=ALU.subtract))
                each(lambda d: OP("dve", "tensor_tensor", [d["bqn"], b_cs, d["bqr"]], [d["bt1"]], out=views(d)[4], in0=views(d)[0], in1=views(d)[3], op=ALU.mult))
                each(lambda d: OP("pool", "tensor_tensor", [d["bqn"], b_cs, d["bqr"]], [d["bt2"]], out=views(d)[5], in0=views(d)[1], in1=views(d)[2], op=ALU.mult))
                each(lambda d: OP("dve", "tensor_tensor", [d["bt1"], d["bt2"]], [d["bqr"]], out=views(d)[6][:, :, :, 1], in0=views(d)[4], in1=views(d)[5], op=ALU.add))
                for d_ in L:
                    tstate[d_["tt"]] = (d_["qr"], d_["bqr"])

            def stC(tt):
                qr, b_qr = tstate.pop(tt)
                tb_ = 6 + (tt % 2)
                pT = ps[tb_].bitcast(BF16)
                for j in range(5):
                    PE_T(pT[:, j * 128:(j + 1) * 128], qr[:, j * 128:(j + 1) * 128], identb, [b_qr, b_identb], [pb[tb_]])
                ACT([pb[tb_]], [b_QT[tt]], out=QTv[:, :, tt * 128:(tt + 1) * 128],
                    in_=pT[:, 0:512].rearrange("p (j t) -> p j t", j=4), func=AF.Copy)
                ACT([pb[tb_]], [b_KT[tt]], out=KT[:, tt * 128:(tt + 1) * 128], in_=pT[:, 512:640], func=AF.Copy)

            nbat = NT // NBT
            for step in range(nbat + 2):
                if step < nbat:
                    for k_ in range(NBT):
                        stA_nt(step * NBT + k_)
                    for k_ in range(NBT):
                        stA_mm(step * NBT + k_)
                if 0 <= step - 1 < nbat:
                    stB([(step - 1) * NBT + k_ for k_ in range(NBT)])
                if 0 <= step - 2 < nbat:
                    for k_ in range(NBT):
                        stC((step - 2) * NBT + k_)

            groups = [(qb, j, kc) for qb in range(4) for j in range(4) for kc in range(NT)]
            LOOK = 1
            pend = {}
            PT2_ring = Ring([A.bf16(1024) for _ in range(3)], "PT2")

            def emit_S(idx):
                qb, j, kc = groups[idx]
                qs = slice(qb * 512, (qb + 1) * 512)
                b0 = 2 * (idx % 2)
                for kv in range(2):
                    rows = slice(kv * 64, kv * 64 + 64)
                    PE_MM(ps[b0 + kv], KT[rows, kc * 128:(kc + 1) * 128], QTv[rows, j, qs], True, True,
                          [b_KT[kc]] + b_QT[qb * 4:qb * 4 + 4], [pb[b0 + kv]])
                PTt, b_PT = PT2_ring.next()
                ACT([pb[b0], pb[b0 + 1]], [b_PT], out=PTt, in_=psall[:, b0 * 512:(b0 + 2) * 512], func=AF.Exp, scale=0.125)
                pend[idx] = (PTt, b_PT)

            def emit_PV(idx):
                qb, j, kc = groups[idx]
                qs = slice(qb * 512, (qb + 1) * 512)
                PTt, b_PT = pend.pop(idx)
                acc0 = 4 + 2 * ((qb * 4 + j) % 2)
                for kv in range(2):
                    accb = acc0 + kv
                    PE_MM(ps[accb], Vav[:, kc, kv, :], PTt[:, kv * 512:(kv + 1) * 512], kc == 0, kc == NT - 1,
                          [b_Va[kc], b_PT], [pb[accb]])
                if kc == NT - 1:
                    for kv in range(2):
                        accb = acc0 + kv
                        rows = slice(kv * 64, kv * 64 + 64)
                        OP("dve", "reciprocal", [pb[accb]], [b_Rr], out=Rr[0:64], in_=ps[accb][64:128, :])
                        OP("dve", "tensor_tensor", [pb[accb], b_Rr], [b_attT[qb]], out=attTv[rows, j, qs], in0=ps[accb][0:64, :],
                           in1=Rr[0:64], op=ALU.mult)

            for idx in range(len(groups) + LOOK):
                if idx < len(groups):
                    emit_S(idx)
                if idx >= LOOK:
                    emit_PV(idx - LOOK)
            P.barrier()
            A.pop()

            if stage == "mix_b":
                OP("dve", "tensor_copy", b_attT + bX, bX, out=X[:, 0:4 * S], in_=attT)
                P.barrier()
                A.pop()
                return
            A.push()
            hTb2 = [A.bf16(8 * 512), Wq[:, 0:8 * 512]]
            hTbv2 = [h_.rearrange("p (k t) -> p k t", k=8) for h_ in hTb2]
            b_hTb2 = [[Buf() for _ in range(4)] for _ in range(2)]
            mg2 = [A.bf16(8 * 512) for _ in range(2)]
            mgv2 = [m_.rearrange("p (k t) -> p k t", k=8) for m_ in mg2]
            b_mg2 = [[Buf() for _ in range(8)] for _ in range(2)]
            Wo = A.bf16(8 * D); Wov = Wo.rearrange("p (k n) -> p k n", k=8); b_WoK = [Buf() for _ in range(8)]
            wga_ring = Ring([A.bf16(1024) for _ in range(2)], "wga")
            wgm_ring = Ring([A.bf16(1024) for _ in range(2)], "wgm")
            wua_ring = Ring([A.bf16(512) for _ in range(2)], "wua")
            wum_ring = Ring([A.bf16(512) for _ in range(2)], "wum")
            sga_ring = Ring([A.f32(512) for _ in range(2)], "sga")
            sgm_ring = Ring([A.f32(512) for _ in range(2)], "sgm")

            def merge_nt(tb):
                for i in range(4):
                    norm_transpose(tb * 4 + i, xn_ring, 6 + (i % 2), hTbv2[tb % 2][:, :, i * 128:(i + 1) * 128], b_hTb2[tb % 2][i])

            def merge_outproj(tb):
                mgv, b_mg = mgv2[tb % 2], b_mg2[tb % 2]
                for i in range(4):
                    tt = tb * 4 + i
                    for half in range(2):
                        yb = (2 * i + half) % 4
                        for kc in range(8):
                            PE_MM(ps[yb][:, :], mgv[:, kc, i * 128:(i + 1) * 128], Wov[:, kc, half * 512:(half + 1) * 512],
                                  kc == 0, kc == 7, [b_mg[kc], b_WoK[kc]], [pb[yb]])
                        xs = Xv[:, tt, half * 512:(half + 1) * 512]
                        OP("dve", "tensor_tensor", [pb[yb], bX[tt]], [bX[tt]], out=xs, in0=xs, in1=ps[yb][:, :], op=ALU.add)
                        if half == 1:
                            ss_square(tt)

            merge_nt(0)
            for tb in range(4):
                ts_ = slice(tb * 512, (tb + 1) * 512)
                mgv, b_mg = mgv2[tb % 2], b_mg2[tb % 2]
                hTbv, b_hTb = hTbv2[tb % 2], b_hTb2[tb % 2]
                for oc in range(8):
                    wga, bwga = wga_ring.next(); wgm, bwgm = wgm_ring.next()
                    wua, bwua = wua_ring.next(); wum, bwum = wum_ring.next()
                    wgav = wga.rearrange("p (k n) -> p k n", k=8); wgmv = wgm.rearrange("p (k n) -> p k n", k=8)
                    wuav = wua.rearrange("p (k n) -> p k n", k=4); wumv = wum.rearrange("p (k n) -> p k n", k=4)
                    load_cast(wg_d[oc], [128, 1024], wga, [], [bwga])
                    load_cast(wg_d[8 + oc], [128, 1024], wgm, [], [bwgm])
                    load_cast(wua_d[oc], [128, 512], wua, [], [bwua])
                    load_cast(wum_d[oc], [128, 512], wum, [], [bwum])
                    if tb == 0 and oc == 1:
                        for kc_ in range(8):
                            load_cast(wout_d[kc_ * 128:(kc_ + 1) * 128, :], [128, D], Wov[:, kc_, :], [], [b_WoK[kc_]])
                    bs = 4 * (oc % 2)
                    for kc in range(8):
                        PE_MM(ps[bs + 0], wgav[:, kc, :], hTbv[:, kc, :], kc == 0, kc == 7, [bwga] + b_hTb, [pb[bs + 0]])
                    for kc in range(8):
                        PE_MM(ps[bs + 1], wgmv[:, kc, :], hTbv[:, kc, :], kc == 0, kc == 7, [bwgm] + b_hTb, [pb[bs + 1]])
                    for k4 in range(4):
                        PE_MM(ps[bs + 2], wuav[:, k4, :], attTv[:, k4, ts_], k4 == 0, k4 == 3, [bwua, b_attT[tb]], [pb[bs + 2]])
                    for k4 in range(4):
                        PE_MM(ps[bs + 3], wumv[:, k4, :], memTv[:, k4, ts_], k4 == 0, k4 == 3,
                              [bwum] + b_memT[tb * 4:tb * 4 + 4], [pb[bs + 3]])
                    sga, bsga = sga_ring.next(); sgm, bsgm = sgm_ring.next()
                    ACT([pb[bs + 0]], [bsga], out=sga, in_=ps[bs + 0], func=AF.Sigmoid)
                    ACT([pb[bs + 1]], [bsgm], out=sgm, in_=ps[bs + 1], func=AF.Sigmoid)
                    OP("dve", "tensor_tensor", [bsga, pb[bs + 2]], [bsga], out=sga, in0=sga, in1=ps[bs + 2], op=ALU.mult)
                    OP("dve", "tensor_tensor", [bsgm, pb[bs + 3]], [bsgm], out=sgm, in0=sgm, in1=ps[bs + 3], op=ALU.mult)
                    OP("dve", "tensor_tensor", [bsga, bsgm], [b_mg[oc]], out=mgv[:, oc, :], in0=sga, in1=sgm, op=ALU.add)
                    if oc == 1 and tb > 0:
                        merge_outproj(tb - 1)
                    if oc == 4 and tb + 1 < 4:
                        merge_nt(tb + 1)
            merge_outproj(3)
            P.barrier()
            A.pop()
            A.pop()

        def load_x(s_, tt):
            P.dma("sp", Xv[:, tt, :], x_d[s_, tt * 128:(tt + 1) * 128, :], w=[bX[tt]])
            ss_square(tt)

        for tt in range(NT):
            load_x(0, tt)
        for s in range(nseq):
            if stage in ("full", "ffn1", "mix", "mix_a", "mix_b") and not noffn:
                ffn(n1_d, w1g_d, w1u_d, w1d_d)
            if stage in ("full", "mix", "mix_a", "mix_b"):
                mixer()
            if stage == "full":
                def final_cb(tt, s=s):
                    ACT([b_ss[tt]], [b_rstd[tt]], out=rstd_all[:, tt:tt + 1], in_=ss_all[:, tt:tt + 1], func=AF.Sqrt,
                        scale=1.0 / D, bias=EPS)
                    OP("dve", "reciprocal", [b_rstd[tt]], [b_rstd[tt]], out=rstd_all[:, tt:tt + 1], in_=rstd_all[:, tt:tt + 1])
                    OP("dve", "memset", [], [b_ss[tt]], ap=ss_all[:, tt:tt + 1], constant=0.0)
                    yo, byo = yo_ring.next()
                    OP("dve", "scalar_tensor_tensor", [bX[tt], b_rstd[tt], b_gfin], [byo], out=yo, in0=Xv[:, tt, :],
                       scalar=rstd_all[:, tt:tt + 1], in1=gfin, op0=ALU.mult, op1=ALU.mult)
                    P.dma("sp", y_d[s, tt * 128:(tt + 1) * 128, :], yo, r=[byo])
                    if s + 1 < nseq:
                        load_x(s + 1, tt)
                ffn(n2_d, w2g_d, w2u_d, w2d_d, final_cb=final_cb)
            else:
                for tt in range(NT):
                    P.dma("sp", y_d[s, tt * 128:(tt + 1) * 128, :], Xv[:, tt, :], r=[bX[tt]])
                P.barrier()
                if s + 1 < nseq:
                    for tt in range(NT):
                        load_x(s + 1, tt)
        P.finish()
        print("program: n_ops=%d arena_peak=%d words" % (P.n_ops, A.peak))
    return nc


def host_constants():
    c = {}
    c["ident"] = np.eye(128, dtype=np.float32)
    rows = S // 64
    row = np.repeat(np.arange(rows, dtype=np.float64), 64)
    col = np.tile(np.arange(64, dtype=np.float64), rows)
    inv_freq = 10000.0 ** (-np.arange(0, 32, 2, dtype=np.float64) / 32.0)
    ang = np.concatenate([row[:, None] * inv_freq, col[:, None] * inv_freq], axis=-1)
    c["cos"] = np.cos(ang).astype(np.float32)
    c["sin"] = np.sin(ang).astype(np.float32)
    i = np.arange(128)
    Uf = (i[:, None] <= i[None, :]).astype(np.float32)
    Ub = (i[:, None] >= i[None, :]).astype(np.float32)
    c["tri"] = np.stack([Uf, Ub, np.ones((128, 128), np.float32)]).astype(np.float32)
    c["msk"] = np.stack([Uf, Ub]).astype(np.float32)
    return c


def make_in_maps(inputs, nseq=SEQ_PER_CORE, ncores=NCORES):
    f = lambda a: np.ascontiguousarray(np.asarray(a, dtype=np.float32))
    xall = np.concatenate([f(inputs["x_prompt"]), f(inputs["x_sample"])], axis=0)
    w_in = f(inputs["w_in"])[0]
    aq, ak, av = w_in[:, 0:512], w_in[:, 512:640], w_in[:, 640:768]
    mq, mk, mv = w_in[:, 768:1280], w_in[:, 1280:1792], w_in[:, 1792:2304]
    gates, opre = w_in[:, 2304:2320], w_in[:, 2320:2832]
    gatt, gml = w_in[:, 2832:3856], w_in[:, 3856:4880]
    hperm = [0, 4, 1, 5, 2, 6, 3, 7]
    aqp = np.concatenate([aq[:, h * 64:(h + 1) * 64] for h in hperm], axis=1)
    gperm = [0, 1, 2, 3, 8, 9, 10, 11, 4, 5, 6, 7, 12, 13, 14, 15]
    def tile_cols(W, ncols):
        K_, N_ = W.shape
        return np.ascontiguousarray(W.reshape(K_ // 128, 128, N_ // ncols, ncols).transpose(2, 1, 0, 3)
                                    .reshape(N_ // ncols, 128, (K_ // 128) * ncols))

    wm_all = np.concatenate([mq, mk, mv, opre], axis=1)
    shared = {
        "w1g": tile_cols(f(inputs["ffn1_w_gate"])[0], 128), "w1u": tile_cols(f(inputs["ffn1_w_up"])[0], 128),
        "w1d": f(inputs["ffn1_w_down"])[0],
        "w2g": tile_cols(f(inputs["ffn2_w_gate"])[0], 128), "w2u": tile_cols(f(inputs["ffn2_w_up"])[0], 128),
        "w2d": f(inputs["ffn2_w_down"])[0],
        "n1": f(inputs["ffn1_norm"])[0], "nm": f(inputs["mix_norm"])[0], "n2": f(inputs["ffn2_norm"])[0],
        "nf": f(inputs["final_norm"]),
        "wqkv": tile_cols(np.concatenate([aqp, ak, av], axis=1), 128),
        "wm": tile_cols(wm_all, 128),
        "wmg": tile_cols(np.ascontiguousarray(gates[:, gperm]), 16)[0],
        "wg": tile_cols(np.concatenate([gatt, gml], axis=1), 128),
        "wua": tile_cols(np.concatenate(
            [f(inputs["w_up_att"])[0][h * 64:(h + 1) * 64] for h in hperm], axis=0), 128),
        "wum": tile_cols(f(inputs["w_up_mlstm"])[0], 128), "wout": f(inputs["w_out"])[0],
        "gq": f(inputs["q_norm"])[0], "gk": f(inputs["k_norm"])[0],
        "gb": np.ascontiguousarray(f(inputs["mlstm_gate_bias"])[0][gperm]),
        "hn": f(inputs["mlstm_head_norm"])[0],
    }
    shared.update(host_constants())
    maps = []
    for c in range(ncores):
        m = dict(shared)
        m["x"] = np.ascontiguousarray(xall[c * nseq:(c + 1) * nseq])
        maps.append(m)
    return maps


def kernel(**inputs):
    nc = build_program(stage=os.environ.get("KSTAGE", "full"))
    maps = make_in_maps(inputs)
    res = run_bass_kernel_spmd(nc, maps, core_ids=list(range(NCORES)))
    yall = np.concatenate([np.asarray(r["y"], dtype=np.float32) for r in res.results], axis=0)
    nb = np.asarray(inputs["x_prompt"]).shape[0]
    return (np.ascontiguousarray(yall[:nb]), np.ascontiguousarray(yall[nb:]))
```

```python
import contextlib
import os
import numpy as np
import concourse.bass as bass
import concourse.mybir as mybir
from concourse.bass_utils import run_bass_kernel_spmd

F32 = mybir.dt.float32
BF16 = mybir.dt.bfloat16
AF = mybir.ActivationFunctionType
ALU = mybir.AluOpType
AX = mybir.AxisListType

D = 1024
S = 2048
NT = S // 128
DFF = 2816
NFC = DFF // 128
EPS = 1e-6
NCORES = 8
SEQ_PER_CORE = 3


class Buf:
    __slots__ = ("w", "r", "name", "excl")

    def __init__(self, name="", excl=False):
        self.w = None
        self.r = {}
        self.name = name
        self.excl = excl


class _Q:
    def __init__(self, name, sem, is_pe=False):
        self.name = name
        self.sem = sem
        self.cnt = 0
        self.ops = []
        self.seen = {}
        self.is_pe = is_pe
        self.dma_sems = []
        self.dma_uses = []
        self.dma_rr = 0


class Prog:
    def __init__(self, nc, stack):
        self.nc = nc
        self.sems = {}
        self.q = {}
        for name, is_pe in (("pe", True), ("act", False), ("dve", False), ("pool", False), ("sp", False)):
            self.sems["s_" + name] = stack.enter_context(nc.semaphore("s_" + name))
            self.q[name] = _Q(name, "s_" + name, is_pe)
        for qn, n in (("sp", 8), ("pool", 16)):
            q = self.q[qn]
            for i in range(n):
                key = "d_%s%d" % (qn, i)
                self.sems[key] = stack.enter_context(nc.semaphore(key))
                q.dma_sems.append(key)
                q.dma_uses.append(0)
        self.n_ops = 0

    def _need(self, q, ev, waits):
        if ev is None:
            return
        key, val = ev
        if q.is_pe and key == q.sem:
            return
        if q.seen.get(key, 0) >= val:
            return
        if waits.get(key, 0) < val:
            waits[key] = val

    def _emit_waits(self, q, r, w, extra=()):
        waits = {}
        for b in r:
            self._need(q, b.w, waits)
            if b.excl:
                for key, val in b.r.items():
                    if key != q.sem:
                        self._need(q, (key, val), waits)
        for b in w:
            self._need(q, b.w, waits)
            for key, val in b.r.items():
                self._need(q, (key, val), waits)
        for ev in extra:
            self._need(q, ev, waits)
        for key, val in waits.items():
            q.seen[key] = val
            sem = self.sems[key]
            q.ops.append(lambda e, sem=sem, val=val: e.wait_ge(sem, val))
            self.n_ops += 1

    def _record(self, ev, r, w):
        key, val = ev
        for b in r:
            if b.r.get(key, 0) < val:
                b.r[key] = val
        for b in w:
            b.w = ev
            b.r = {}

    def op(self, qname, fn, r=(), w=()):
        q = self.q[qname]
        self._emit_waits(q, r, w)
        q.cnt += 1
        sem = self.sems[q.sem]
        q.ops.append(lambda e, fn=fn, sem=sem: fn(e).then_inc(sem, 1))
        self.n_ops += 1
        ev = (q.sem, q.cnt)
        self._record(ev, r, w)
        return ev

    def dma(self, qname, out, in_, r=(), w=()):
        q = self.q[qname]
        i = q.dma_rr
        q.dma_rr = (i + 1) % len(q.dma_sems)
        key = q.dma_sems[i]
        prev = (key, 16 * q.dma_uses[i]) if q.dma_uses[i] else None
        self._emit_waits(q, r, w, extra=(prev,) if prev else ())
        q.dma_uses[i] += 1
        sem = self.sems[key]
        q.ops.append(lambda e, out=out, in_=in_, sem=sem: e.dma_start(out=out, in_=in_).then_inc(sem, 16))
        self.n_ops += 1
        ev = (key, 16 * q.dma_uses[i])
        self._record(ev, r, w)
        return ev

    def all_events(self):
        evs = []
        for q in self.q.values():
            if q.cnt:
                evs.append((q.sem, q.cnt))
            for key, u in zip(q.dma_sems, q.dma_uses):
                if u:
                    evs.append((key, 16 * u))
        return evs

    def barrier(self, qnames=("pe", "act", "dve", "pool", "sp")):
        evs = self.all_events()
        for qn in qnames:
            q = self.q[qn]
            waits = {}
            for ev in evs:
                if ev[0] == q.sem:
                    continue
                self._need(q, ev, waits)
            for key, val in waits.items():
                q.seen[key] = val
                sem = self.sems[key]
                q.ops.append(lambda e, sem=sem, val=val: e.wait_ge(sem, val))

    def finish(self):
        self.barrier(("sp", "pool"))
        nc = self.nc
        with nc.Block() as block:
            @block.tensor
            def _(e):
                for f in self.q["pe"].ops:
                    f(e)

            @block.scalar
            def _(e):
                for f in self.q["act"].ops:
                    f(e)

            @block.vector
            def _(e):
                for f in self.q["dve"].ops:
                    f(e)

            @block.gpsimd
            def _(e):
                for f in self.q["pool"].ops:
                    f(e)

            @block.sync
            def _(e):
                for f in self.q["sp"].ops:
                    f(e)


class Arena:
    def __init__(self, t, words):
        self.t = t
        self.words = words
        self.off = 0
        self.marks = []
        self.peak = 0

    def f32(self, n):
        assert self.off + n <= self.words, ("sbuf arena overflow", self.off, n, self.words)
        ap = self.t[:, self.off:self.off + n]
        self.off += n
        self.peak = max(self.peak, self.off)
        return ap

    def bf16(self, n):
        w = (n + 1) // 2
        assert self.off + w <= self.words, ("sbuf arena overflow", self.off, w, self.words)
        ap = self.t[:, self.off:self.off + w].bitcast(BF16)
        self.off += w
        self.peak = max(self.peak, self.off)
        return ap

    def push(self):
        self.marks.append(self.off)

    def pop(self):
        self.off = self.marks.pop()


class Ring:
    def __init__(self, aps, name):
        self.aps = aps
        self.bufs = [Buf("%s%d" % (name, i)) for i in range(len(aps))]
        self.i = 0

    def next(self):
        i = self.i
        self.i = (i + 1) % len(self.aps)
        return self.aps[i], self.bufs[i]


ARENA_WORDS = 53200


def build_program(nseq=SEQ_PER_CORE, stage="full"):
    noffn = stage.startswith("only_")
    if noffn:
        stage = "mix_" + stage[5:]
    nc = bass.Bass("TRN2", target_bir_lowering=False)

    def din(name, shape):
        return nc.dram_tensor(name, list(shape), F32, kind="ExternalInput").ap()

    x_d = din("x", [nseq, S, D])
    y_d = nc.dram_tensor("y", [nseq, S, D], F32, kind="ExternalOutput").ap()
    w1g_d, w1u_d, w1d_d = din("w1g", [NFC, 128, 1024]), din("w1u", [NFC, 128, 1024]), din("w1d", [DFF, D])
    w2g_d, w2u_d, w2d_d = din("w2g", [NFC, 128, 1024]), din("w2u", [NFC, 128, 1024]), din("w2d", [DFF, D])
    n1_d, nm_d, n2_d, nf_d = din("n1", [D]), din("nm", [D]), din("n2", [D]), din("nf", [D])
    wqkv_d = din("wqkv", [6, 128, 1024])
    wm_d = din("wm", [16, 128, 1024])
    wmg_d = din("wmg", [128, 128])
    wg_d = din("wg", [16, 128, 1024])
    wua_d, wum_d, wout_d = din("wua", [8, 128, 512]), din("wum", [8, 128, 512]), din("wout", [D, D])
    gq_d, gk_d, gb_d, hn_d = din("gq", [64]), din("gk", [64]), din("gb", [16]), din("hn", [512])
    ident_d = din("ident", [128, 128])
    cos_d, sin_d = din("cos", [S, 32]), din("sin", [S, 32])
    tri_d = din("tri", [3, 128, 128])
    msk_d = din("msk", [2, 128, 128])

    with contextlib.ExitStack() as st:
        P = Prog(nc, st)
        arena_t = st.enter_context(nc.sbuf_tensor("arena", [128, ARENA_WORDS], F32))
        A = Arena(arena_t, ARENA_WORDS)
        psall = st.enter_context(nc.psum_tensor("psall", [128, 4096], F32))
        ps = [psall[:, i * 512:(i + 1) * 512] for i in range(8)]
        pb = [Buf("ps%d" % i, excl=True) for i in range(8)]

        def PE_MM(out, lhsT, rhs, start, stop, r, w):
            P.op("pe", lambda e: e.matmul(out, lhsT=lhsT, rhs=rhs, start=start, stop=stop), r, w)

        def PE_T(out, in_, ident, r, w):
            P.op("pe", lambda e: e.transpose(out=out, in_=in_, identity=ident), r, w)

        def ACT(r, w, **kw):
            P.op("act", lambda e: e.activation(**kw), r, w)

        def OP(qn, name, r, w, **kw):
            P.op(qn, lambda e: getattr(e, name)(**kw), r, w)

        X = A.f32(NT * D)
        Xv = X.rearrange("p (t d) -> p t d", t=NT)
        bX = [Buf("X%d" % i) for i in range(NT)]
        identf = A.f32(128); b_identf = Buf()
        identb = A.bf16(128); b_identb = Buf()
        cos_t = A.f32(NT * 32); sin_t = A.f32(NT * 32); b_cs = Buf()
        tri = A.f32(3 * 128); b_tri = Buf()
        triv = tri.rearrange("p (a b) -> p a b", a=3)
        msk = A.f32(2 * 128); b_msk = Buf()
        mskv = msk.rearrange("p (a b) -> p a b", a=2)
        gbc = A.f32(D); b_gbc = Buf()
        gq_bc = A.f32(64); gk_bc = A.f32(64); gb_bc = A.f32(16); hn_bc = A.f32(512); b_small = Buf()
        gfin = A.f32(D); b_gfin = Buf()
        yo_ring = Ring([A.f32(D) for _ in range(2)], "yo")
        rstd_all = A.f32(NT); b_rstd = [Buf() for _ in range(NT)]
        ss_all = A.f32(NT); b_ss = [Buf() for _ in range(NT)]
        ones_bf = A.bf16(2); b_ones = Buf()
        junk = A.bf16(D); b_junk = Buf()

        scr_blk = nc.dram_tensor("scr_blk", [2, NT, 128, 1536], BF16).ap()
        scr_vo = nc.dram_tensor("scr_vo", [NT, 128, 1024], BF16).ap()
        b_scr_blk = [[Buf() for _ in range(NT)] for _ in range(2)]
        b_scr_vo = [Buf() for _ in range(NT)]
        dec_all = A.f32(2 * NT * 4); dec_allv = dec_all.rearrange("p (d t h) -> p d t h", d=2, t=NT)
        b_dec = [[Buf() for _ in range(NT)] for _ in range(2)]
        P.dma("sp", identf, ident_d[:, :], w=[b_identf])
        OP("dve", "tensor_copy", [b_identf], [b_identb], out=identb, in_=identf)
        P.dma("sp", cos_t.rearrange("p (t j) -> p t j", t=NT), cos_d.rearrange("(t p) j -> p t j", p=128), w=[b_cs])
        P.dma("sp", sin_t.rearrange("p (t j) -> p t j", t=NT), sin_d.rearrange("(t p) j -> p t j", p=128), w=[b_cs])
        P.dma("sp", triv, tri_d.rearrange("a p b -> p a b"), w=[b_tri])
        P.dma("sp", mskv, msk_d.rearrange("a p b -> p a b"), w=[b_msk])
        P.dma("sp", gq_bc, gq_d.partition_broadcast(128), w=[b_small])
        P.dma("sp", gk_bc, gk_d.partition_broadcast(128), w=[b_small])
        P.dma("sp", gb_bc, gb_d.partition_broadcast(128), w=[b_small])
        P.dma("sp", hn_bc, hn_d.partition_broadcast(128), w=[b_small])
        OP("dve", "memset", [], [b_ones], ap=ones_bf, constant=1.0)
        P.dma("sp", gfin, nf_d.partition_broadcast(128), w=[b_gfin])
        gqk_bc = A.f32(640); b_gqk = Buf()
        gqkv = gqk_bc.rearrange("p (h d) -> p h d", h=10)
        OP("dve", "tensor_copy", [b_small], [b_gqk], out=gqkv[:, 0:8, :], in_=gq_bc.unsqueeze(1).broadcast_to([128, 8, 64]))
        OP("dve", "tensor_copy", [b_small, b_gqk], [b_gqk], out=gqkv[:, 8:10, :], in_=gk_bc.unsqueeze(1).broadcast_to([128, 2, 64]))
        OP("dve", "memset", [], b_ss, ap=ss_all, constant=0.0)
        cosv = cos_t.rearrange("p (t j) -> p t j", t=NT)
        sinv = sin_t.rearrange("p (t j) -> p t j", t=NT)

        def load_cast(dram_ap, shape, dst, r_dst, w_dst):
            P.dma("pool", dst, dram_ap, r=list(r_dst), w=list(w_dst))

        def colblock(w_d, c0, ncols):
            return w_d[:, c0:c0 + ncols].rearrange("(kc p) n -> p kc n", p=128)

        def load_gain(g_d):
            P.dma("sp", gbc, g_d.partition_broadcast(128), w=[b_gbc])

        def ss_square(tt):
            ACT([bX[tt], b_ss[tt]], [b_junk, b_ss[tt]], out=junk, in_=Xv[:, tt, :], func=AF.Square,
                accum_out=ss_all[:, tt:tt + 1])

        def rstd_finish():
            ACT(b_ss, b_rstd, out=rstd_all, in_=ss_all, func=AF.Sqrt, scale=1.0 / D, bias=EPS)
            OP("dve", "reciprocal", b_rstd, b_rstd, out=rstd_all, in_=rstd_all)
            OP("dve", "memset", [], b_ss, ap=ss_all, constant=0.0)

        def norm_transpose(tt, xn_ring, tp_bank, dst, b_dst, evac="act"):
            xn, bxn = xn_ring.next()
            OP("dve", "scalar_tensor_tensor", [bX[tt], b_rstd[tt], b_gbc], [bxn], out=xn, in0=Xv[:, tt, :],
               scalar=rstd_all[:, tt:tt + 1], in1=gbc, op0=ALU.mult, op1=ALU.mult)
            pT = ps[tp_bank][:, :].bitcast(BF16)
            for kc in range(8):
                PE_T(pT[:, kc * 128:(kc + 1) * 128], xn[:, kc * 128:(kc + 1) * 128], identb,
                     [bxn, b_identb], [pb[tp_bank]])
            if evac == "act":
                ACT([pb[tp_bank]], [b_dst], out=dst, in_=pT.rearrange("p (k t) -> p k t", k=8), func=AF.Copy)
            else:
                OP("dve", "tensor_copy", [pb[tp_bank]], [b_dst], out=dst,
                   in_=pT.rearrange("p (k t) -> p k t", k=8))

        def ffn(g_d, wg_dd, wu_dd, wd_dd, final_cb=None):
            A.push()
            load_gain(g_d)
            NG, CG = 2, NFC // 2
            xnT = A.bf16(8 * 1024); xnTv = xnT.rearrange("p (k t) -> p k t", k=8)
            b_xnT = [Buf() for _ in range(8)]
            hT = A.bf16(CG * 1024); hTv = hT.rearrange("p (c t) -> p c t", c=CG)
            b_hT = [[Buf() for _ in range(2)] for _ in range(CG)]
            WdR = [A.bf16(CG * D) for _ in range(2)]
            WdRv = [w.rearrange("p (c n) -> p c n", c=CG) for w in WdR]
            b_Wd = [[Buf() for _ in range(CG)] for _ in range(2)]
            wg_ring = Ring([A.bf16(1024) for _ in range(4)], "wg")
            wu_ring = Ring([A.bf16(1024) for _ in range(4)], "wu")
            xn_ring = Ring([A.bf16(D) for _ in range(2)], "xn")
            sg_ring = Ring([A.f32(512) for _ in range(2)], "sg")
            rstd_finish()
            it = 0
            for sbk in range(2):
                for i in range(8):
                    norm_transpose(sbk * 8 + i, xn_ring, 6 + (i % 2), xnTv[:, :, i * 128:(i + 1) * 128], b_xnT[i])
                for g in range(NG):
                    wsel = (sbk * NG + g) % 2
                    for cl in range(CG):
                        c = g * CG + cl
                        wg, bwg = wg_ring.next()
                        wu, bwu = wu_ring.next()
                        wgv = wg.rearrange("p (k n) -> p k n", k=8)
                        wuv = wu.rearrange("p (k n) -> p k n", k=8)
                        load_cast(wg_dd[c], [128, 1024], wg, [], [bwg])
                        load_cast(wu_dd[c], [128, 1024], wu, [], [bwu])
                        load_cast(wd_dd[c * 128:(c + 1) * 128, :], [128, D], WdRv[wsel][:, cl, :], [], [b_Wd[wsel][cl]])
                        for th in range(2):
                            gb_, ub_ = it % 2, 2 + it % 2
                            it += 1
                            tsl = slice(th * 512, (th + 1) * 512)
                            for kc in range(8):
                                PE_MM(ps[gb_][:, :], wgv[:, kc, :], xnTv[:, kc, tsl], kc == 0, kc == 7,
                                      [bwg] + b_xnT[th * 4:th * 4 + 4], [pb[gb_]])
                            for kc in range(8):
                                PE_MM(ps[ub_][:, :], wuv[:, kc, :], xnTv[:, kc, tsl], kc == 0, kc == 7,
                                      [bwu] + b_xnT[th * 4:th * 4 + 4], [pb[ub_]])
                            sg, bsg = sg_ring.next()
                            ACT([pb[gb_]], [bsg], out=sg, in_=ps[gb_][:, :], func=AF.Silu)
                            OP("dve", "tensor_tensor", [bsg, pb[ub_]], [b_hT[cl][th]], out=hTv[:, cl, tsl], in0=sg,
                               in1=ps[ub_][:, :], op=ALU.mult)
                    for i in range(8):
                        tt = sbk * 8 + i
                        for half in range(2):
                            yb = 4 + half
                            for cl in range(CG):
                                PE_MM(ps[yb][:, :], hTv[:, cl, i * 128:(i + 1) * 128],
                                      WdRv[wsel][:, cl, half * 512:(half + 1) * 512],
                                      cl == 0, cl == CG - 1, [b_hT[cl][i // 4], b_Wd[wsel][cl]], [pb[yb]])
                            xs = Xv[:, tt, half * 512:(half + 1) * 512]
                            OP("dve", "scalar_tensor_tensor", [pb[yb], bX[tt]], [bX[tt]], out=xs, in0=ps[yb][:, :],
                               scalar=0.5, in1=xs, op0=ALU.mult, op1=ALU.add)
                            if g == NG - 1 and half == 1:
                                ss_square(tt)
                                if final_cb is not None:
                                    final_cb(tt)
            P.barrier()
            A.pop()

        def mixer():
            A.push()
            load_gain(nm_d)
            rstd_finish()
            memT = A.bf16(4 * S); memTv = memT.rearrange("p (h t) -> p h t", h=4)
            b_memT = [Buf() for _ in range(NT)]
            xn_ring = Ring([A.bf16(D) for _ in range(2)], "xn")
            Wq = A.bf16(6 * 1024); Wqb = Wq.rearrange("p (c k n) -> p c k n", c=6, k=8); b_WqB = [Buf() for _ in range(6)]

            def h4(ap):
                return ap.rearrange("p (h e) -> p h e", h=4)

            A.push()
            if stage == "mix_b":
                OP("dve", "memset", [], b_memT, ap=memT, constant=0.0)
            Wm = A.bf16(16 * 1024); Wmb = Wm.rearrange("p (c k n) -> p c k n", c=16, k=8)
            Wmg = A.bf16(128); Wmgv = Wmg.rearrange("p (k n) -> p k n", k=8)
            b_WmB = [Buf() for _ in range(17)]
            b_WmJ = [b_WmB[4 * j_:4 * j_ + 4] for j_ in range(4)] + [[b_WmB[16]]]
            for cb in range(16):
                load_cast(wm_d[cb], [128, 1024], Wm[:, cb * 1024:(cb + 1) * 1024], [], [b_WmB[cb]])
                if cb == 7:
                    load_cast(wmg_d, [128, 128], Wmg, [], [b_WmB[16]])
            for cb in range(6):
                load_cast(wqkv_d[cb], [128, 1024], Wq[:, cb * 1024:(cb + 1) * 1024], [], [b_WqB[cb]])
            hTt_ring = Ring([A.bf16(1024) for _ in range(3)], "hTt")
            blk_ring = [Ring([A.bf16(1536) for _ in range(2)], "blk%d" % d_) for d_ in range(2)]
            vo_ring = Ring([A.bf16(1024) for _ in range(2)], "vo")
            G = A.f32(16); b_G = Buf()
            sp8 = A.f32(8); b_sp8 = Buf()
            cums = A.f32(16); b_cums = Buf()
            agb = A.f32(24); b_agb = Buf()
            tmp8 = A.f32(8); b_tmp8 = Buf()
            QaK = [[A.bf16(512) for _ in range(2)] for _ in range(2)]
            b_QaK = [[Buf() for _ in range(2)] for _ in range(2)]
            smallp = ps[6]
            pp = {}

            hts = {}

            def ppNT(tt):
                hTt, b_hTt = hTt_ring.next()
                hTtv = hTt.rearrange("p (k t) -> p k t", k=8)
                hts[tt] = (hTtv, b_hTt)
                norm_transpose(tt, xn_ring, 7, hTtv, b_hTt, evac="dve")

            def ppA1(tt):
                hTtv, b_hTt = hts[tt]
                qb_, kb_ = 2 * (tt % 2), 2 * (tt % 2) + 1
                for j, bank in ((0, qb_), (1, kb_)):
                    for kc in range(8):
                        PE_MM(ps[bank], hTtv[:, kc, :], Wmb[:, 4 * j:4 * j + 4, kc, :], kc == 0, kc == 7,
                              [b_hTt] + b_WmJ[j], [pb[bank]])

            def ppA2(tt):
                hTtv, b_hTt = hts.pop(tt)
                for j, bank in ((2, 4), (3, 5)):
                    for kc in range(8):
                        PE_MM(ps[bank], hTtv[:, kc, :], Wmb[:, 4 * j:4 * j + 4, kc, :], kc == 0, kc == 7,
                              [b_hTt] + b_WmJ[j], [pb[bank]])
                for kc in range(8):
                    PE_MM(smallp[:, 0:16], hTtv[:, kc, :], Wmgv[:, kc, :], kc == 0, kc == 7,
                          [b_hTt] + b_WmJ[4], [pb[6]])

            vos = {}

            def ppBg(tt):
                OP("dve", "tensor_tensor", [pb[6], b_small], [b_G], out=G, in0=smallp[:, 0:16], in1=gb_bc, op=ALU.add)
                ACT([b_G], [b_sp8], out=sp8, in_=G[:, 8:16], func=AF.Exp, scale=-1.0)
                ACT([b_sp8], [b_sp8], out=sp8, in_=sp8, func=AF.Ln, bias=1.0)
                for d_ in range(2):
                    PE_MM(smallp[:, 16 + 4 * d_:20 + 4 * d_], triv[:, d_, :], sp8[:, 4 * d_:4 * d_ + 4], True, True,
                          [b_tri, b_sp8], [pb[6]])
                PE_MM(smallp[:, 24:32], triv[:, 2, :], sp8, True, True, [b_tri, b_sp8], [pb[6]])
                OP("dve", "tensor_copy", [pb[6]], [b_cums], out=cums, in_=smallp[:, 16:32])
                ACT([b_cums], [b_agb], out=agb[:, 0:8], in_=cums[:, 0:8], func=AF.Exp, scale=-1.0)
                OP("dve", "tensor_tensor", [b_G, b_cums], [b_tmp8], out=tmp8, in0=G[:, 0:8], in1=cums[:, 0:8], op=ALU.add)
                ACT([b_tmp8], [b_agb], out=agb[:, 8:16], in_=tmp8, func=AF.Exp)
                OP("dve", "tensor_tensor", [b_tmp8, b_cums], [b_tmp8], out=tmp8, in0=tmp8, in1=cums[:, 8:16], op=ALU.subtract)
                ACT([b_tmp8], [b_agb], out=agb[:, 16:24], in_=tmp8, func=AF.Exp)
                for d_ in range(2):
                    ACT([b_cums], [b_dec[d_][tt]], out=dec_allv[:, d_, tt, :], in_=cums[:, 8 + 4 * d_:12 + 4 * d_], func=AF.Exp, scale=-1.0)

            def ppB(tt):
                qb_, kb_ = 2 * (tt % 2), 2 * (tt % 2) + 1
                vo, b_vo = vo_ring.next()
                ACT([pb[4]], [b_vo], out=vo[:, 0:512], in_=ps[4], func=AF.Copy)
                ACT([pb[5]], [b_vo], out=vo[:, 512:1024], in_=ps[5], func=AF.Sigmoid)
                blks = []
                for d_ in range(2):
                    blk, b_blk = blk_ring[d_].next()
                    blks.append((blk, b_blk))
                    al = agb[:, 4 * d_:4 * d_ + 4]; gm = agb[:, 8 + 4 * d_:12 + 4 * d_]; be = agb[:, 16 + 4 * d_:20 + 4 * d_]
                    OP("dve", "tensor_tensor", [pb[qb_], b_agb], [b_QaK[d_][0]], out=h4(QaK[d_][0]), in0=h4(ps[qb_]),
                       in1=al.unsqueeze(2).broadcast_to([128, 4, 128]), op=ALU.mult)
                    OP("dve", "scalar_tensor_tensor", [pb[kb_], b_agb], [b_QaK[d_][1]], out=h4(QaK[d_][1]), in0=h4(ps[kb_]),
                       scalar=128.0 ** -0.5, in1=gm.unsqueeze(2).broadcast_to([128, 4, 128]), op0=ALU.mult, op1=ALU.mult)
                    OP("dve", "scalar_tensor_tensor", [pb[kb_], b_agb], [b_blk], out=h4(blk[:, 1024:1536]), in0=h4(ps[kb_]),
                       scalar=128.0 ** -0.5, in1=be.unsqueeze(2).broadcast_to([128, 4, 128]), op0=ALU.mult, op1=ALU.mult)
                pp[tt] = (vo, b_vo, blks)

            def ppC(tt):
                vo, b_vo, blks = pp.pop(tt)
                pT = ps[7].bitcast(BF16)
                for d_ in range(2):
                    blk, b_blk = blks[d_]
                    for h in range(4):
                        PE_T(pT[:, h * 128:(h + 1) * 128], QaK[d_][0][:, h * 128:(h + 1) * 128], identb, [b_QaK[d_][0], b_identb], [pb[7]])
                    for h in range(4):
                        PE_T(pT[:, 512 + h * 128:512 + (h + 1) * 128], QaK[d_][1][:, h * 128:(h + 1) * 128], identb, [b_QaK[d_][1], b_identb], [pb[7]])
                    OP("dve", "tensor_copy", [pb[7]], [b_blk], out=blk[:, 0:1024], in_=pT[:, 0:1024])
                    P.dma("sp", scr_blk[d_, tt], blk, r=[b_blk], w=[b_scr_blk[d_][tt]])
                P.dma("sp", scr_vo[tt], vo, r=[b_vo], w=[b_scr_vo[tt]])

            if stage != "mix_b":
                ppNT(0)
                ppNT(1)
                ppA1(0)
                ppA2(0)
                for tt in range(NT):
                    if tt + 1 < NT:
                        ppA1(tt + 1)
                    ppBg(tt)
                    if tt + 2 < NT:
                        ppNT(tt + 2)
                    ppB(tt)
                    if tt + 1 < NT:
                        ppA2(tt + 1)
                    ppC(tt)
            P.barrier()
            A.pop()

            A.push()
            Hf = A.f32(NT * 512); Hfv = Hf.rearrange("p (t h e) -> p t h e", t=NT, h=4)
            b_Hf = [Buf() for _ in range(NT)]
            small = ps[3]
            NSL = 3

            def mk_dir():
                dd = {}
                dd["C"] = A.f32(512); dd["bC"] = Buf()
                dd["Cb"] = A.bf16(512); dd["bCb"] = Buf()
                dd["n"] = A.f32(4); dd["bn"] = Buf()
                dd["nb"] = A.bf16(4); dd["bnb"] = Buf()
                dd["SmT"] = A.bf16(512); dd["bSmT"] = Buf()
                dd["rden"] = A.f32(4); dd["brden"] = Buf()
                dd["slots"] = []
                for _ in range(NSL):
                    so = {"blk": A.bf16(1536), "bblk": Buf(), "vo": A.bf16(1024), "bvo": Buf()}
                    dd["slots"].append(so)
                return dd

            dirs = [mk_dir(), mk_dir()]
            sq2 = A.f32(512); b_sq2 = Buf()
            dirs[0]["hsc"] = sq2; dirs[0]["bhsc"] = b_sq2
            dirs[1]["hsc"] = A.f32(512); dirs[1]["bhsc"] = Buf()
            ssh = A.f32(4); b_ssh = Buf()
            hs2 = sq2; b_hs2 = b_sq2
            memt = A.bf16(512); b_memt = Buf()

            def fetch(dirn, tt, slot):
                so = dirs[dirn]["slots"][slot]
                P.dma("sp", so["blk"], scr_blk[dirn, tt], r=[b_scr_blk[dirn][tt]], w=[so["bblk"]])
                P.dma("sp", so["vo"], scr_vo[tt], r=[b_scr_vo[tt]], w=[so["bvo"]])

            dbanks = [(5, 6, 7, 3, 32), (0, 1, 2, 4, 256)]

            def rec_pe1(dirn, slot):
                dd = dirs[dirn]; so = dd["slots"][slot]
                bST = dbanks[dirn][0]
                QaTv, KgTv = h4(so["blk"][:, 0:512]), h4(so["blk"][:, 512:1024])
                for h in range(4):
                    PE_MM(ps[bST][:, h * 128:(h + 1) * 128], KgTv[:, h, :], QaTv[:, h, :], True, True,
                          [so["bblk"]], [pb[bST]])
                OP("dve", "tensor_tensor", [pb[bST], b_msk], [dd["bSmT"]], out=h4(dd["SmT"]), in0=h4(ps[bST]),
                   in1=mskv[:, dirn, :].unsqueeze(1).broadcast_to([128, 4, 128]), op=ALU.mult)

            def rec_pe2(dirn, slot, tt, first):
                dd = dirs[dirn]; so = dd["slots"][slot]
                _, bNUM, bDC, bSM, dn0 = dbanks[dirn]
                smallb = ps[bSM]
                QaTv, Kbv, Vtv = h4(so["blk"][:, 0:512]), h4(so["blk"][:, 1024:1536]), h4(so["vo"][:, 0:512])
                SmTv = h4(dd["SmT"])
                Cbv = h4(dd["Cb"]); Cstv = h4(dd["C"])
                nb, nst, rden = dd["nb"], dd["n"], dd["rden"]
                for h in range(4):
                    PE_MM(ps[bDC][:, h * 128:(h + 1) * 128], Kbv[:, h, :], Vtv[:, h, :], True, True,
                          [so["bblk"], so["bvo"]], [pb[bDC]])
                    PE_MM(smallb[:, dn0 + 4 + h:dn0 + 5 + h], Kbv[:, h, :], ones_bf[:, 0:1], True, True,
                          [so["bblk"], b_ones], [pb[bSM]])
                for h in range(4):
                    PE_MM(ps[bNUM][:, h * 128:(h + 1) * 128], SmTv[:, h, :], Vtv[:, h, :], True, False,
                          [dd["bSmT"], so["bvo"]], [pb[bNUM]])
                    PE_MM(ps[bNUM][:, h * 128:(h + 1) * 128], QaTv[:, h, :], Cbv[:, h, :], False, True,
                          [so["bblk"], dd["bCb"]], [pb[bNUM]])
                    PE_MM(smallb[:, dn0 + h:dn0 + h + 1], SmTv[:, h, :], ones_bf[:, 0:1], True, False,
                          [dd["bSmT"], b_ones], [pb[bSM]])
                    PE_MM(smallb[:, dn0 + h:dn0 + h + 1], QaTv[:, h, :], nb[:, h:h + 1], False, True,
                          [so["bblk"], dd["bnb"]], [pb[bSM]])
                dec = dec_allv[:, dirn, tt, :]
                OP("pool", "tensor_tensor", [dd["bC"], b_dec[dirn][tt], dd["bCb"]], [dd["bC"]], out=Cstv, in0=Cstv,
                   in1=dec.unsqueeze(2).broadcast_to([128, 4, 128]), op=ALU.mult)
                ACT([pb[bSM]], [dd["brden"]], out=rden, in_=smallb[:, dn0:dn0 + 4], func=AF.Abs)
                OP("dve", "tensor_scalar_max", [dd["brden"]], [dd["brden"]], out=rden, in0=rden, scalar1=1.0)
                OP("dve", "reciprocal", [dd["brden"]], [dd["brden"]], out=rden, in_=rden)
                numv = h4(ps[bNUM])
                rb = rden.unsqueeze(2).broadcast_to([128, 4, 128])
                if first:
                    OP("dve", "tensor_tensor", [pb[bNUM], dd["brden"]], [b_Hf[tt]], out=Hfv[:, tt], in0=numv, in1=rb, op=ALU.mult)
                else:
                    hscv = h4(dd["hsc"])
                    OP("dve", "tensor_tensor", [pb[bNUM], dd["brden"]], [dd["bhsc"]], out=hscv, in0=numv, in1=rb, op=ALU.mult)
                    OP("pool", "tensor_tensor", [dd["bhsc"], b_Hf[tt]], [b_Hf[tt]], out=Hfv[:, tt], in0=Hfv[:, tt],
                       in1=hscv, op=ALU.add)
                OP("dve", "tensor_tensor", [dd["bC"], pb[bDC]], [dd["bC"]], out=dd["C"], in0=dd["C"], in1=ps[bDC], op=ALU.add)
                OP("dve", "tensor_tensor", [dd["bn"], b_dec[dirn][tt]], [dd["bn"]], out=nst, in0=nst, in1=dec, op=ALU.mult)
                OP("dve", "tensor_tensor", [dd["bn"], pb[bSM]], [dd["bn"]], out=nst, in0=nst, in1=smallb[:, dn0 + 4:dn0 + 8], op=ALU.add)
                ACT([dd["bC"]], [dd["bCb"]], out=dd["Cb"], in_=dd["C"], func=AF.Copy)
                OP("dve", "tensor_copy", [dd["bn"]], [dd["bnb"]], out=nb, in_=nst)

            def finalize(dirn, slot, tt):
                dd = dirs[dirn]; so = dd["slots"][slot]
                ACT([b_Hf[tt]], [b_sq2], out=sq2, in_=Hf[:, tt * 512:(tt + 1) * 512], func=AF.Square)
                OP("dve", "tensor_reduce", [b_sq2], [b_ssh], out=ssh, in_=h4(sq2), axis=AX.X, op=ALU.add)
                ACT([b_ssh], [b_ssh], out=ssh, in_=ssh, func=AF.Sqrt, scale=1.0 / 128, bias=EPS)
                OP("dve", "reciprocal", [b_ssh], [b_ssh], out=ssh, in_=ssh)
                OP("dve", "tensor_tensor", [b_Hf[tt], b_ssh], [b_hs2], out=h4(hs2), in0=Hfv[:, tt],
                   in1=ssh.unsqueeze(2).broadcast_to([128, 4, 128]), op=ALU.mult)
                OP("pool", "tensor_tensor", [b_hs2, b_small], [b_hs2], out=hs2, in0=hs2, in1=hn_bc, op=ALU.mult)
                OP("dve", "tensor_tensor", [b_hs2, so["bvo"]], [b_memt], out=memt, in0=hs2, in1=so["vo"][:, 512:1024], op=ALU.mult)
                pT2 = ps[4].bitcast(BF16)
                for h in range(4):
                    PE_T(pT2[:, h * 128:(h + 1) * 128], memt[:, h * 128:(h + 1) * 128], identb, [b_memt, b_identb], [pb[4]])
                OP("dve", "tensor_copy", [pb[4]], [b_memT[tt]], out=memTv[:, :, tt * 128:(tt + 1) * 128],
                   in_=pT2[:, 0:512].rearrange("p (h t) -> p h t", h=4))

            if stage != "mix_b":
                for dd in dirs:
                    OP("dve", "memset", [], [dd["bC"]], ap=dd["C"], constant=0.0)
                    OP("dve", "memset", [], [dd["bCb"]], ap=dd["Cb"], constant=0.0)
                    OP("dve", "memset", [], [dd["bn"]], ap=dd["n"], constant=0.0)
                    OP("dve", "memset", [], [dd["bnb"]], ap=dd["nb"], constant=0.0)
                tile_of = lambda dirn, i: i if dirn == 0 else NT - 1 - i
                for pre in range(NSL - 1):
                    fetch(0, tile_of(0, pre), pre % NSL)
                    fetch(1, tile_of(1, pre), pre % NSL)
                for i in range(NT):
                    slot = i % NSL
                    nx = i + NSL - 1
                    if nx < NT:
                        fetch(0, tile_of(0, nx), nx % NSL)
                        fetch(1, tile_of(1, nx), nx % NSL)
                    first = i < NT // 2
                    tf, tb_ = tile_of(0, i), tile_of(1, i)
                    rec_pe1(0, slot)
                    rec_pe1(1, slot)
                    rec_pe2(0, slot, tf, first)
                    rec_pe2(1, slot, tb_, first)
                    if not first:
                        finalize(0, slot, tf)
                        finalize(1, slot, tb_)
            P.barrier()
            A.pop()

            if stage == "mix_a":
                OP("dve", "tensor_copy", b_memT + bX, bX, out=X[:, 0:4 * S], in_=memT)
                P.barrier()
                A.pop()
                return
            attT = A.bf16(4 * S); attTv = attT.rearrange("p (h t) -> p h t", h=4)
            b_attT = [Buf() for _ in range(4)]
            A.push()
            QT = A.bf16(4 * S); QTv = QT.rearrange("p (j t) -> p j t", j=4); b_QT = [Buf() for _ in range(NT)]
            KT = A.bf16(S); b_KT = [Buf() for _ in range(NT)]
            Va = A.bf16(NT * 256); Vav = Va.rearrange("p (t k e) -> p t k e", t=NT, k=2); b_Va = [Buf() for _ in range(NT)]
            OP("dve", "memset", [], b_Va, ap=Va, constant=1.0)
            NBT = 2
            hTt_ring = Ring([A.bf16(1024) for _ in range(2 * NBT)], "hTt")
            sqq_r = Ring([A.f32(640) for _ in range(NBT)], "sqq")
            ssq_r = Ring([A.f32(10) for _ in range(NBT)], "ssq")
            qn_r = Ring([A.f32(640) for _ in range(NBT)], "qn")
            t1_r = Ring([A.f32(320) for _ in range(NBT)], "t1")
            t2_r = Ring([A.f32(320) for _ in range(NBT)], "t2")
            qr_r = Ring([A.bf16(640) for _ in range(2 * NBT)], "qr")
            Rr = A.f32(512); b_Rr = Buf()
            tstate = {}

            def kvslot(tt):
                return 4 + ((tt // 2) % 2), (tt % 2) * 256

            hta = {}

            def stA_nt(tt):
                hTt, b_hTt = hTt_ring.next()
                hTtv = hTt.rearrange("p (k t) -> p k t", k=8)
                hta[tt] = (hTtv, b_hTt)
                norm_transpose(tt, xn_ring, 6 + (tt % 2), hTtv, b_hTt, evac="dve")

            def stA_mm(tt):
                hTtv, b_hTt = hta.pop(tt)
                qb_ = tt % 4
                kvb, kvo = kvslot(tt)
                for kc in range(8):
                    PE_MM(ps[qb_], hTtv[:, kc, :], Wqb[:, 0:4, kc, :], kc == 0, kc == 7, [b_hTt] + b_WqB[0:4], [pb[qb_]])
                for kc in range(8):
                    PE_MM(ps[kvb][:, kvo:kvo + 256], hTtv[:, kc, :], Wqb[:, 4:6, kc, :], kc == 0, kc == 7, [b_hTt] + b_WqB[4:6], [pb[kvb]])

            def stB(tts):
                L = []
                for tt in tts:
                    d_ = {"tt": tt, "qb": tt % 4}
                    d_["kvb"], d_["kvo"] = kvslot(tt)
                    for nm, rg in (("sqq", sqq_r), ("ssq", ssq_r), ("qn", qn_r), ("t1", t1_r), ("t2", t2_r), ("qr", qr_r)):
                        d_[nm], d_["b" + nm] = rg.next()
                    L.append(d_)

                def each(fn):
                    for d_ in L:
                        fn(d_)
                each(lambda d: ACT([pb[d["kvb"]]], [b_Va[d["tt"]]], out=Vav[:, d["tt"], :, 0:64],
                                   in_=ps[d["kvb"]][:, d["kvo"] + 128:d["kvo"] + 256].rearrange("p (k e) -> p k e", k=2), func=AF.Copy))
                each(lambda d: ACT([pb[d["qb"]]], [d["bsqq"]], out=d["sqq"][:, 0:512], in_=ps[d["qb"]], func=AF.Square))
                each(lambda d: ACT([pb[d["kvb"]]], [d["bsqq"]], out=d["sqq"][:, 512:640], in_=ps[d["kvb"]][:, d["kvo"]:d["kvo"] + 128], func=AF.Square))
                each(lambda d: OP("dve", "tensor_reduce", [d["bsqq"]], [d["bssq"]], out=d["ssq"], in_=d["sqq"].rearrange("p (h d) -> p h d", h=10), axis=AX.X, op=ALU.add))
                each(lambda d: ACT([d["bssq"]], [d["bssq"]], out=d["ssq"], in_=d["ssq"], func=AF.Sqrt, scale=1.0 / 64, bias=EPS))
                each(lambda d: OP("dve", "reciprocal", [d["bssq"]], [d["bssq"]], out=d["ssq"], in_=d["ssq"]))
                each(lambda d: OP("dve", "tensor_tensor", [pb[d["qb"]], d["bssq"]], [d["bqn"]], out=d["qn"].rearrange("p (h d) -> p h d", h=10)[:, 0:8, :],
                                  in0=ps[d["qb"]].rearrange("p (h d) -> p h d", h=8),
                                  in1=d["ssq"][:, 0:8].unsqueeze(2).broadcast_to([128, 8, 64]), op=ALU.mult))
                each(lambda d: OP("dve", "tensor_tensor", [pb[d["kvb"]], d["bssq"]], [d["bqn"]], out=d["qn"].rearrange("p (h d) -> p h d", h=10)[:, 8:10, :],
                                  in0=ps[d["kvb"]][:, d["kvo"]:d["kvo"] + 128].rearrange("p (h d) -> p h d", h=2),
                                  in1=d["ssq"][:, 8:10].unsqueeze(2).broadcast_to([128, 2, 64]), op=ALU.mult))
                each(lambda d: OP("dve", "tensor_tensor", [d["bqn"], b_gqk], [d["bqn"]], out=d["qn"], in0=d["qn"], in1=gqk_bc, op=ALU.mult))

                def views(d):
                    q4 = d["qn"].rearrange("p (h j two) -> p h j two", h=10, two=2)
                    cb_ = cosv[:, d["tt"], :].unsqueeze(1).broadcast_to([128, 10, 32])
                    sb_ = sinv[:, d["tt"], :].unsqueeze(1).broadcast_to([128, 10, 32])
                    t1v = d["t1"].rearrange("p (h j) -> p h j", h=10); t2v = d["t2"].rearrange("p (h j) -> p h j", h=10)
                    qr4 = d["qr"].rearrange("p (h j two) -> p h j two", h=10, two=2)
                    return q4[:, :, :, 0], q4[:, :, :, 1], cb_, sb_, t1v, t2v, qr4
                each(lambda d: OP("dve", "tensor_tensor", [d["bqn"], b_cs], [d["bt1"]], out=views(d)[4], in0=views(d)[0], in1=views(d)[2], op=ALU.mult))
                each(lambda d: OP("dve", "tensor_tensor", [d["bqn"], b_cs], [d["bt2"]], out=views(d)[5], in0=views(d)[1], in1=views(d)[3], op=ALU.mult))
                each(lambda d: OP("dve", "tensor_tensor", [d["bt1"], d["bt2"]], [d["bqr"]], out=views(d)[6][:, :, :, 0], in0=views(d)[4], in1=views(d)[5], op=ALU.subtract))
                each(lambda d: OP("dve", "tensor_tensor", [d["bqn"], b_cs, d["bqr"]], [d["bt1"]], out=views(d)[4], in0=views(d)[0], in1=views(d)[3], op=ALU.mult))
                each(lambda d: OP("dve", "tensor_tensor", [d["bqn"], b_cs, d["bqr"]], [d["bt2"]], out=views(d)[5], in0=views(d)[1], in1=views(d)[2], op=ALU.mult))
                each(lambda d: OP("dve", "tensor_tensor", [d["bt1"], d["bt2"]], [d["bqr"]], out=views(d)[6][:, :, :, 1], in0=views(d)[4], in1=views(d)[5], op=ALU.add))
                for d_ in L:
                    tstate[d_["tt"]] = (d_["qr"], d_["bqr"])

            def stC(tt):
                qr, b_qr = tstate.pop(tt)
                tb_ = 6 + (tt % 2)
                pT = ps[tb_].bitcast(BF16)
                for j in range(5):
                    PE_T(pT[:, j * 128:(j + 1) * 128], qr[:, j * 128:(j + 1) * 128], identb, [b_qr, b_identb], [pb[tb_]])
                ACT([pb[tb_]], [b_QT[tt]], out=QTv[:, :, tt * 128:(tt + 1) * 128],
                    in_=pT[:, 0:512].rearrange("p (j t) -> p j t", j=4), func=AF.Copy)
                ACT([pb[tb_]], [b_KT[tt]], out=KT[:, tt * 128:(tt + 1) * 128], in_=pT[:, 512:640], func=AF.Copy)

            nbat = NT // NBT
            for step in range(nbat + 2):
                if step < nbat:
                    for k_ in range(NBT):
                        stA_nt(step * NBT + k_)
                    for k_ in range(NBT):
                        stA_mm(step * NBT + k_)
                if 0 <= step - 1 < nbat:
                    stB([(step - 1) * NBT + k_ for k_ in range(NBT)])
                if 0 <= step - 2 < nbat:
                    for k_ in range(NBT):
                        stC((step - 2) * NBT + k_)

            groups = [(qb, j, kc) for qb in range(4) for j in range(4) for kc in range(NT)]
            LOOK = 1
            pend = {}
            PT2_ring = Ring([A.bf16(1024) for _ in range(3)], "PT2")

            def emit_S(idx):
                qb, j, kc = groups[idx]
                qs = slice(qb * 512, (qb + 1) * 512)
                b0 = 2 * (idx % 2)
                for kv in range(2):
                    rows = slice(kv * 64, kv * 64 + 64)
                    PE_MM(ps[b0 + kv], KT[rows, kc * 128:(kc + 1) * 128], QTv[rows, j, qs], True, True,
                          [b_KT[kc]] + b_QT[qb * 4:qb * 4 + 4], [pb[b0 + kv]])
                PTt, b_PT = PT2_ring.next()
                ACT([pb[b0], pb[b0 + 1]], [b_PT], out=PTt, in_=psall[:, b0 * 512:(b0 + 2) * 512], func=AF.Exp, scale=0.125)
                pend[idx] = (PTt, b_PT)

            def emit_PV(idx):
                qb, j, kc = groups[idx]
                qs = slice(qb * 512, (qb + 1) * 512)
                PTt, b_PT = pend.pop(idx)
                acc0 = 4 + 2 * ((qb * 4 + j) % 2)
                for kv in range(2):
                    accb = acc0 + kv
                    PE_MM(ps[accb], Vav[:, kc, kv, :], PTt[:, kv * 512:(kv + 1) * 512], kc == 0, kc == NT - 1,
                          [b_Va[kc], b_PT], [pb[accb]])
                if kc == NT - 1:
                    for kv in range(2):
                        accb = acc0 + kv
                        rows = slice(kv * 64, kv * 64 + 64)
                        OP("dve", "reciprocal", [pb[accb]], [b_Rr], out=Rr[0:64], in_=ps[accb][64:128, :])
                        OP("dve", "tensor_tensor", [pb[accb], b_Rr], [b_attT[qb]], out=attTv[rows, j, qs], in0=ps[accb][0:64, :],
                           in1=Rr[0:64], op=ALU.mult)

            for idx in range(len(groups) + LOOK):
                if idx < len(groups):
                    emit_S(idx)
                if idx >= LOOK:
                    emit_PV(idx - LOOK)
            P.barrier()
            A.pop()

            if stage == "mix_b":
                OP("dve", "tensor_copy", b_attT + bX, bX, out=X[:, 0:4 * S], in_=attT)
                P.barrier()
                A.pop()
                return
            A.push()
            hTb2 = [A.bf16(8 * 512), Wq[:, 0:8 * 512]]
            hTbv2 = [h_.rearrange("p (k t) -> p k t", k=8) for h_ in hTb2]
            b_hTb2 = [[Buf() for _ in range(4)] for _ in range(2)]
            mg2 = [A.bf16(8 * 512) for _ in range(2)]
            mgv2 = [m_.rearrange("p (k t) -> p k t", k=8) for m_ in mg2]
            b_mg2 = [[Buf() for _ in range(8)] for _ in range(2)]
            Wo = A.bf16(8 * D); Wov = Wo.rearrange("p (k n) -> p k n", k=8); b_WoK = [Buf() for _ in range(8)]
            wga_ring = Ring([A.bf16(1024) for _ in range(2)], "wga")
            wgm_ring = Ring([A.bf16(1024) for _ in range(2)], "wgm")
            wua_ring = Ring([A.bf16(512) for _ in range(2)], "wua")
            wum_ring = Ring([A.bf16(512) for _ in range(2)], "wum")
            sga_ring = Ring([A.f32(512) for _ in range(2)], "sga")
            sgm_ring = Ring([A.f32(512) for _ in range(2)], "sgm")

            def merge_nt(tb):
                for i in range(4):
                    norm_transpose(tb * 4 + i, xn_ring, 6 + (i % 2), hTbv2[tb % 2][:, :, i * 128:(i + 1) * 128], b_hTb2[tb % 2][i])

            def merge_outproj(tb):
                mgv, b_mg = mgv2[tb % 2], b_mg2[tb % 2]
                for i in range(4):
                    tt = tb * 4 + i
                    for half in range(2):
                        yb = (2 * i + half) % 4
                        for kc in range(8):
                            PE_MM(ps[yb][:, :], mgv[:, kc, i * 128:(i + 1) * 128], Wov[:, kc, half * 512:(half + 1) * 512],
                                  kc == 0, kc == 7, [b_mg[kc], b_WoK[kc]], [pb[yb]])
                        xs = Xv[:, tt, half * 512:(half + 1) * 512]
                        OP("dve", "tensor_tensor", [pb[yb], bX[tt]], [bX[tt]], out=xs, in0=xs, in1=ps[yb][:, :], op=ALU.add)
                        if half == 1:
                            ss_square(tt)

            merge_nt(0)
            for tb in range(4):
                ts_ = slice(tb * 512, (tb + 1) * 512)
                mgv, b_mg = mgv2[tb % 2], b_mg2[tb % 2]
                hTbv, b_hTb = hTbv2[tb % 2], b_hTb2[tb % 2]
                for oc in range(8):
                    wga, bwga = wga_ring.next(); wgm, bwgm = wgm_ring.next()
                    wua, bwua = wua_ring.next(); wum, bwum = wum_ring.next()
                    wgav = wga.rearrange("p (k n) -> p k n", k=8); wgmv = wgm.rearrange("p (k n) -> p k n", k=8)
                    wuav = wua.rearrange("p (k n) -> p k n", k=4); wumv = wum.rearrange("p (k n) -> p k n", k=4)
                    load_cast(wg_d[oc], [128, 1024], wga, [], [bwga])
                    load_cast(wg_d[8 + oc], [128, 1024], wgm, [], [bwgm])
                    load_cast(wua_d[oc], [128, 512], wua, [], [bwua])
                    load_cast(wum_d[oc], [128, 512], wum, [], [bwum])
                    if tb == 0 and oc == 1:
                        for kc_ in range(8):
                            load_cast(wout_d[kc_ * 128:(kc_ + 1) * 128, :], [128, D], Wov[:, kc_, :], [], [b_WoK[kc_]])
                    bs = 4 * (oc % 2)
                    for kc in range(8):
                        PE_MM(ps[bs + 0], wgav[:, kc, :], hTbv[:, kc, :], kc == 0, kc == 7, [bwga] + b_hTb, [pb[bs + 0]])
                    for kc in range(8):
                        PE_MM(ps[bs + 1], wgmv[:, kc, :], hTbv[:, kc, :], kc == 0, kc == 7, [bwgm] + b_hTb, [pb[bs + 1]])
                    for k4 in range(4):
                        PE_MM(ps[bs + 2], wuav[:, k4, :], attTv[:, k4, ts_], k4 == 0, k4 == 3, [bwua, b_attT[tb]], [pb[bs + 2]])
                    for k4 in range(4):
                        PE_MM(ps[bs + 3], wumv[:, k4, :], memTv[:, k4, ts_], k4 == 0, k4 == 3,
                              [bwum] + b_memT[tb * 4:tb * 4 + 4], [pb[bs + 3]])
                    sga, bsga = sga_ring.next(); sgm, bsgm = sgm_ring.next()
                    ACT([pb[bs + 0]], [bsga], out=sga, in_=ps[bs + 0], func=AF.Sigmoid)
                    ACT([pb[bs + 1]], [bsgm], out=sgm, in_=ps[bs + 1], func=AF.Sigmoid)
                    OP("dve", "tensor_tensor", [bsga, pb[bs + 2]], [bsga], out=sga, in0=sga, in1=ps[bs + 2], op=ALU.mult)
                    OP("dve", "tensor_tensor", [bsgm, pb[bs + 3]], [bsgm], out=sgm, in0=sgm, in1=ps[bs + 3], op=ALU.mult)
                    OP("dve", "tensor_tensor", [bsga, bsgm], [b_mg[oc]], out=mgv[:, oc, :], in0=sga, in1=sgm, op=ALU.add)
                    if oc == 1 and tb > 0:
                        merge_outproj(tb - 1)
                    if oc == 4 and tb + 1 < 4:
                        merge_nt(tb + 1)
            merge_outproj(3)
            P.barrier()
            A.pop()
            A.pop()

        def load_x(s_, tt):
            P.dma("sp", Xv[:, tt, :], x_d[s_, tt * 128:(tt + 1) * 128, :], w=[bX[tt]])
            ss_square(tt)

        for tt in range(NT):
            load_x(0, tt)
        for s in range(nseq):
            if stage in ("full", "ffn1", "mix", "mix_a", "mix_b") and not noffn:
                ffn(n1_d, w1g_d, w1u_d, w1d_d)
            if stage in ("full", "mix", "mix_a", "mix_b"):
                mixer()
            if stage == "full":
                def final_cb(tt, s=s):
                    ACT([b_ss[tt]], [b_rstd[tt]], out=rstd_all[:, tt:tt + 1], in_=ss_all[:, tt:tt + 1], func=AF.Sqrt,
                        scale=1.0 / D, bias=EPS)
                    OP("dve", "reciprocal", [b_rstd[tt]], [b_rstd[tt]], out=rstd_all[:, tt:tt + 1], in_=rstd_all[:, tt:tt + 1])
                    OP("dve", "memset", [], [b_ss[tt]], ap=ss_all[:, tt:tt + 1], constant=0.0)
                    yo, byo = yo_ring.next()
                    OP("dve", "scalar_tensor_tensor", [bX[tt], b_rstd[tt], b_gfin], [byo], out=yo, in0=Xv[:, tt, :],
                       scalar=rstd_all[:, tt:tt + 1], in1=gfin, op0=ALU.mult, op1=ALU.mult)
                    P.dma("sp", y_d[s, tt * 128:(tt + 1) * 128, :], yo, r=[byo])
                    if s + 1 < nseq:
                        load_x(s + 1, tt)
                ffn(n2_d, w2g_d, w2u_d, w2d_d, final_cb=final_cb)
            else:
                for tt in range(NT):
                    P.dma("sp", y_d[s, tt * 128:(tt + 1) * 128, :], Xv[:, tt, :], r=[bX[tt]])
                P.barrier()
                if s + 1 < nseq:
                    for tt in range(NT):
                        load_x(s + 1, tt)
        P.finish()
        print("program: n_ops=%d arena_peak=%d words" % (P.n_ops, A.peak))
    return nc


def host_constants():
    c = {}
    c["ident"] = np.eye(128, dtype=np.float32)
    rows = S // 64
    row = np.repeat(np.arange(rows, dtype=np.float64), 64)
    col = np.tile(np.arange(64, dtype=np.float64), rows)
    inv_freq = 10000.0 ** (-np.arange(0, 32, 2, dtype=np.float64) / 32.0)
    ang = np.concatenate([row[:, None] * inv_freq, col[:, None] * inv_freq], axis=-1)
    c["cos"] = np.cos(ang).astype(np.float32)
    c["sin"] = np.sin(ang).astype(np.float32)
    i = np.arange(128)
    Uf = (i[:, None] <= i[None, :]).astype(np.float32)
    Ub = (i[:, None] >= i[None, :]).astype(np.float32)
    c["tri"] = np.stack([Uf, Ub, np.ones((128, 128), np.float32)]).astype(np.float32)
    c["msk"] = np.stack([Uf, Ub]).astype(np.float32)
    return c


def make_in_maps(inputs, nseq=SEQ_PER_CORE, ncores=NCORES):
    f = lambda a: np.ascontiguousarray(np.asarray(a, dtype=np.float32))
    xall = np.concatenate([f(inputs["x_prompt"]), f(inputs["x_sample"])], axis=0)
    w_in = f(inputs["w_in"])[0]
    aq, ak, av = w_in[:, 0:512], w_in[:, 512:640], w_in[:, 640:768]
    mq, mk, mv = w_in[:, 768:1280], w_in[:, 1280:1792], w_in[:, 1792:2304]
    gates, opre = w_in[:, 2304:2320], w_in[:, 2320:2832]
    gatt, gml = w_in[:, 2832:3856], w_in[:, 3856:4880]
    hperm = [0, 4, 1, 5, 2, 6, 3, 7]
    aqp = np.concatenate([aq[:, h * 64:(h + 1) * 64] for h in hperm], axis=1)
    gperm = [0, 1, 2, 3, 8, 9, 10, 11, 4, 5, 6, 7, 12, 13, 14, 15]
    def tile_cols(W, ncols):
        K_, N_ = W.shape
        return np.ascontiguousarray(W.reshape(K_ // 128, 128, N_ // ncols, ncols).transpose(2, 1, 0, 3)
                                    .reshape(N_ // ncols, 128, (K_ // 128) * ncols))

    wm_all = np.concatenate([mq, mk, mv, opre], axis=1)
    shared = {
        "w1g": tile_cols(f(inputs["ffn1_w_gate"])[0], 128), "w1u": tile_cols(f(inputs["ffn1_w_up"])[0], 128),
        "w1d": f(inputs["ffn1_w_down"])[0],
        "w2g": tile_cols(f(inputs["ffn2_w_gate"])[0], 128), "w2u": tile_cols(f(inputs["ffn2_w_up"])[0], 128),
        "w2d": f(inputs["ffn2_w_down"])[0],
        "n1": f(inputs["ffn1_norm"])[0], "nm": f(inputs["mix_norm"])[0], "n2": f(inputs["ffn2_norm"])[0],
        "nf": f(inputs["final_norm"]),
        "wqkv": tile_cols(np.concatenate([aqp, ak, av], axis=1), 128),
        "wm": tile_cols(wm_all, 128),
        "wmg": tile_cols(np.ascontiguousarray(gates[:, gperm]), 16)[0],
        "wg": tile_cols(np.concatenate([gatt, gml], axis=1), 128),
        "wua": tile_cols(np.concatenate(
            [f(inputs["w_up_att"])[0][h * 64:(h + 1) * 64] for h in hperm], axis=0), 128),
        "wum": tile_cols(f(inputs["w_up_mlstm"])[0], 128), "wout": f(inputs["w_out"])[0],
        "gq": f(inputs["q_norm"])[0], "gk": f(inputs["k_norm"])[0],
        "gb": np.ascontiguousarray(f(inputs["mlstm_gate_bias"])[0][gperm]),
        "hn": f(inputs["mlstm_head_norm"])[0],
    }
    shared.update(host_constants())
    maps = []
    for c in range(ncores):
        m = dict(shared)
        m["x"] = np.ascontiguousarray(xall[c * nseq:(c + 1) * nseq])
        maps.append(m)
    return maps


def kernel(**inputs):
    nc = build_program(stage=os.environ.get("KSTAGE", "full"))
    maps = make_in_maps(inputs)
    res = run_bass_kernel_spmd(nc, maps, core_ids=list(range(NCORES)))
    yall = np.concatenate([np.asarray(r["y"], dtype=np.float32) for r in res.results], axis=0)
    nb = np.asarray(inputs["x_prompt"]).shape[0]
    return (np.ascontiguousarray(yall[:nb]), np.ascontiguousarray(yall[nb:]))
```
